# Optimizing a Trainium2 kernel written in Bass

```python
import jax, jax.numpy as jnp
from jax import lax
import numpy as np

D_MODEL = 1024
BATCH = 4
SEQ = 8192
DEPTH = 4

CHUNK = 64
N_MIXERS = 2
POOL_WINDOWS = (2, 4, 8, 16)
POOL_GROUPS = len(POOL_WINDOWS)
POOL_GW = D_MODEL // POOL_GROUPS
N_HEADS = 16
HEAD_DIM = D_MODEL // N_HEADS
LEFT_CHUNKS = 8
BAND_CHUNKS = LEFT_CHUNKS + 1
BAND = BAND_CHUNKS * CHUNK
REL_CLIP = 256
D_FF = 2816
CONV_W = 3
N_A = (DEPTH + 1) // 2
N_B = DEPTH // 2
EPS = 1e-6

kernel_name = "hybrid_pool_chunkattn_convffn"


def rmsnorm(x, g):
    xf = x.astype(jnp.float32)
    y = xf * lax.rsqrt(jnp.mean(xf * xf, axis=-1, keepdims=True) + EPS)
    return (y * g.astype(jnp.float32)).astype(x.dtype)


def pool_mixer(h, w_pool, b_pool, scale):
    B, S, D = h.shape
    hf = h.astype(jnp.float32)
    csp = jnp.concatenate([jnp.zeros((B, 1, D), jnp.float32), jnp.cumsum(hf, axis=1)], axis=1)
    t = jnp.arange(S)
    outs = []
    for g, w in enumerate(POOL_WINDOWS):
        c = csp[:, :, g * POOL_GW:(g + 1) * POOL_GW]
        upper = c[:, 1:]
        lower = jnp.concatenate([jnp.zeros((B, w - 1, POOL_GW), jnp.float32), c[:, :S - w + 1]], axis=1)
        cnt = jnp.minimum(t + 1, w).astype(jnp.float32)[None, :, None]
        outs.append((upper - lower) / cnt)
    pooled = jnp.concatenate(outs, axis=-1)
    y = (pooled - hf).astype(h.dtype).reshape(B, S, POOL_GROUPS, POOL_GW)
    y = jnp.einsum('bsgc,gcd->bsgd', y, w_pool).reshape(B, S, D) + b_pool
    return y * scale


def head_rmsnorm(x, g):
    xf = x.astype(jnp.float32)
    y = xf * lax.rsqrt(jnp.mean(xf * xf, axis=-1, keepdims=True) + EPS)
    return (y * g.astype(jnp.float32)).astype(x.dtype)


def chunk_attention(h, w_qkv, q_norm, k_norm, rel_table, w_o):
    B, S, D = h.shape
    nc = S // CHUNK
    qkv = h @ w_qkv
    q, k, v = jnp.split(qkv, 3, axis=-1)
    q = head_rmsnorm(q.reshape(B, S, N_HEADS, HEAD_DIM), q_norm)
    k = head_rmsnorm(k.reshape(B, S, N_HEADS, HEAD_DIM), k_norm)
    v = v.reshape(B, S, N_HEADS, HEAD_DIM)
    q = q.reshape(B, nc, CHUNK, N_HEADS, HEAD_DIM)
    pad = jnp.zeros((B, LEFT_CHUNKS, CHUNK, N_HEADS, HEAD_DIM), k.dtype)
    kp = jnp.concatenate([pad, k.reshape(B, nc, CHUNK, N_HEADS, HEAD_DIM)], axis=1)
    vp = jnp.concatenate([pad, v.reshape(B, nc, CHUNK, N_HEADS, HEAD_DIM)], axis=1)
    rel = jnp.arange(CHUNK)[:, None] - jnp.arange(BAND)[None, :] + LEFT_CHUNKS * CHUNK
    idx = jnp.clip(rel, -REL_CLIP, REL_CLIP) + REL_CLIP
    bias = rel_table.astype(jnp.float32)[:, idx]
    band_chunk = jnp.repeat(jnp.arange(BAND_CHUNKS), CHUNK)
    scale = HEAD_DIM ** -0.5

    def one_chunk(c):
        qc = lax.dynamic_index_in_dim(q, c, axis=1, keepdims=False)
        kb = lax.dynamic_slice_in_dim(kp, c, BAND_CHUNKS, axis=1).reshape(B, BAND, N_HEADS, HEAD_DIM)
        vb = lax.dynamic_slice_in_dim(vp, c, BAND_CHUNKS, axis=1).reshape(B, BAND, N_HEADS, HEAD_DIM)
        s = jnp.einsum('bqhd,bkhd->bhqk', qc, kb).astype(jnp.float32) * scale + bias[None]
        valid = (band_chunk + c - LEFT_CHUNKS) >= 0
        s = jnp.where(valid[None, None, None, :], s, -1e30)
        p = jax.nn.softmax(s, axis=-1).astype(vb.dtype)
        return jnp.einsum('bhqk,bkhd->bqhd', p, vb)

    o = lax.map(one_chunk, jnp.arange(nc))
    o = jnp.transpose(o, (1, 0, 2, 3, 4)).reshape(B, S, D)
    return o @ w_o


def conv_ffn(h, w_gate, w_val, conv_w, conv_b, w_out):
    S = h.shape[1]
    a = h @ w_gate
    ap = jnp.pad(a, ((0, 0), (CONV_W - 1, 0), (0, 0)))
    a = ap[:, 0:S] * conv_w[0] + ap[:, 1:S + 1] * conv_w[1] + ap[:, 2:S + 2] * conv_w[2] + conv_b
    return (jax.nn.silu(a) * (h @ w_val)) @ w_out


def setup_inputs(seed: int = 0) -> dict:
    key = jax.random.key(seed)
    ks = jax.random.split(key, 16)
    f32 = jnp.float32
    D, F = D_MODEL, D_FF
    nrm = lambda k, shape, s: jax.random.normal(k, shape, f32) * s
    return {
        "x": nrm(ks[0], (BATCH, SEQ, D), 1.0),
        "mix_norm": 1.0 + nrm(ks[1], (DEPTH, D), 0.02),
        "ffn_norm": 1.0 + nrm(ks[2], (DEPTH, D), 0.02),
        "pool_w": nrm(ks[3], (N_A, POOL_GROUPS, POOL_GW, POOL_GW), POOL_GW ** -0.5),
        "pool_b": nrm(ks[4], (N_A, D), 0.02),
        "pool_scale": 1.0 + nrm(ks[5], (N_A, D), 0.1),
        "attn_wqkv": nrm(ks[6], (N_B, D, 3 * D), D ** -0.5),
        "attn_q_norm": 1.0 + nrm(ks[7], (N_B, HEAD_DIM), 0.02),
        "attn_k_norm": 1.0 + nrm(ks[8], (N_B, HEAD_DIM), 0.02),
        "attn_rel_bias": nrm(ks[9], (N_B, N_HEADS, 2 * REL_CLIP + 1), 0.1),
        "attn_wo": nrm(ks[10], (N_B, D, D), D ** -0.5),
        "ffn_w_gate": nrm(ks[11], (DEPTH, D, F), D ** -0.5),
        "ffn_w_val": nrm(ks[12], (DEPTH, D, F), D ** -0.5),
        "ffn_conv_w": nrm(ks[13], (DEPTH, CONV_W, F), CONV_W ** -0.5),
        "ffn_conv_b": nrm(ks[14], (DEPTH, F), 0.02),
        "ffn_w_out": nrm(ks[15], (DEPTH, F, D), F ** -0.5),
    }


def reference(x, mix_norm, ffn_norm, pool_w, pool_b, pool_scale, attn_wqkv, attn_q_norm,
              attn_k_norm, attn_rel_bias, attn_wo, ffn_w_gate, ffn_w_val, ffn_conv_w,
              ffn_conv_b, ffn_w_out):
    for i in range(DEPTH):
        h = rmsnorm(x, mix_norm[i])
        j = i // N_MIXERS
        if i % N_MIXERS == 0:
            x = x + pool_mixer(h, pool_w[j], pool_b[j], pool_scale[j])
        else:
            x = x + chunk_attention(h, attn_wqkv[j], attn_q_norm[j], attn_k_norm[j],
                                    attn_rel_bias[j], attn_wo[j])
        h = rmsnorm(x, ffn_norm[i])
        x = x + conv_ffn(h, ffn_w_gate[i], ffn_w_val[i], ffn_conv_w[i], ffn_conv_b[i], ffn_w_out[i])
    return x
```

```python
import numpy as np
from contextlib import ExitStack
import concourse.bass as bass
import concourse.mybir as mybir
from concourse.bass_utils import run_bass_kernel_spmd

F32 = mybir.dt.float32
F32R = mybir.dt.float32r
BF16 = mybir.dt.bfloat16
AF = mybir.ActivationFunctionType
ALU = mybir.AluOpType


class Buf:
    __slots__ = ("name", "w", "r", "excl")

    def __init__(self, name="", excl=False):
        self.name = name
        self.excl = excl
        self.w = None
        self.r = {}


class Sched:
    ENGS = ("pe", "act", "dve", "pool", "sp")

    def __init__(self, nc, stack):
        self.nc = nc
        self.stack = stack
        self.eng = {"pe": nc.tensor, "act": nc.scalar, "dve": nc.vector,
                    "pool": nc.gpsimd, "sp": nc.sync}
        self.sems = {}
        self.cnt = {}
        self.seen = {e: {} for e in self.ENGS}
        self.vc = {}
        self.snap = {}
        self.order = {}
        self.nissue = 0
        self.nwaits = 0
        self.ekey = {}
        self.nphase = 0
        self.new_phase()

    def new_sem(self, key):
        self.sems[key] = self.stack.enter_context(self.nc.semaphore(key))
        self.cnt[key] = 0
        return key

    def new_phase(self):
        p = self.nphase
        self.nphase += 1
        for e in self.ENGS:
            if e in ("sp", "pool"):
                self.ekey[e] = None
                continue
            self.ekey[e] = self.new_sem(f"e_{e}_{p}")

    def _need(self, e, evs):
        seen = self.seen[e]
        need = {}
        for ev in evs:
            if ev is None:
                continue
            k, v = ev
            if seen.get(k, 0) >= v:
                continue
            if need.get(k, 0) < v:
                need[k] = v
        out = []
        for k, v in sorted(need.items(), key=lambda kv: -self.order.get(kv, 0)):
            if seen.get(k, 0) >= v:
                continue
            out.append((k, v))
            self._learn(e, k, v)
        return out

    def _learn(self, e, k, v):
        seen = self.seen[e]
        if seen.get(k, 0) < v:
            seen[k] = v
        vc = self.vc.get((k, v))
        if vc is not None:
            for kk, vv in vc.items():
                if seen.get(kk, 0) < vv:
                    seen[kk] = vv
        self.snap[e] = None

    def _snapshot(self, e):
        sn = self.snap.get(e)
        if sn is None:
            sn = dict(self.seen[e])
            self.snap[e] = sn
        return sn

    def wait(self, e, evs, keep_one=False):
        need = self._need(e, evs)
        attach = None
        if keep_one and need:
            attach = need.pop()
        for k, v in need:
            assert v <= self.cnt[k] + 1, (k, v, self.cnt[k])
            self.eng[e].wait_ge(self.sems[k], v)
            self.nwaits += 1
        return attach

    def _deps(self, e, reads, writes, is_dma):
        own = self.ekey[e] if (e == "pe" and not is_dma) else None
        evs = []
        for b in reads:
            evs.append(b.w)
            if b.excl:
                for k, v in b.r.items():
                    if k != own:
                        evs.append((k, v))
        for b in writes:
            if b.w is not None and b.w[0] != own:
                evs.append(b.w)
            for k, v in b.r.items():
                if k != own:
                    evs.append((k, v))
        return evs

    def _commit(self, ev, reads, writes):
        k, v = ev
        for b in reads:
            if b.r.get(k, 0) < v:
                b.r[k] = v
        for b in writes:
            b.w = ev
            b.r = {}

    def _issued(self, e, ev):
        self.nissue += 1
        self.order[ev] = self.nissue
        self.vc[ev] = self._snapshot(e)

    def op(self, e, fn, reads=(), writes=(), signal=True):
        att = self.wait(e, self._deps(e, reads, writes, False), keep_one=True)
        ins = fn(self.eng[e])
        if att is not None:
            ins._wait_ge(self.sems[att[0]], att[1])
        k = self.ekey[e]
        if signal:
            ins.then_inc(self.sems[k], 1)
            self.cnt[k] += 1
            ev = (k, self.cnt[k])
        else:
            ev = (k, self.cnt[k] + 1)
        self._issued(e, ev)
        self._commit(ev, reads, writes)
        return ins

    def dma(self, q, out, in_, reads=(), writes=(), sem=None):
        self.wait(q, self._deps(q, reads, writes, True))
        ins = self.eng[q].dma_start(out=out, in_=in_)
        ins.then_inc(self.sems[sem], 16)
        self.cnt[sem] += 16
        self._issued(q, (sem, self.cnt[sem]))
        self._commit((sem, self.cnt[sem]), reads, writes)
        return ins

    def all_events(self):
        return [(k, v) for k, v in self.cnt.items() if v > 0]

    def barrier(self):
        evs = self.all_events()
        for e in self.ENGS:
            self.wait(e, evs)

    def finish(self, e="sp"):
        self.wait(e, self.all_events())

    def I(self, e, meth, reads=(), writes=(), signal=True, **kw):
        att = self.wait(e, self._deps(e, reads, writes, False), keep_one=True)
        ins = getattr(self.eng[e], meth)(**kw)
        if att is not None:
            ins._wait_ge(self.sems[att[0]], att[1])
        k = self.ekey[e]
        if signal:
            ins.then_inc(self.sems[k], 1)
            self.cnt[k] += 1
            ev = (k, self.cnt[k])
        else:
            ev = (k, self.cnt[k] + 1)
        self._issued(e, ev)
        self._commit(ev, reads, writes)
        return ins

    def dsem(self, name):
        if name not in self.sems:
            self.new_sem(name)
        return name


D = 1024
NCH = 8
FF = 2816
NF = 22
TS = 512
WT = 5632
NT = WT // TS
OT = 3
OWN = 4096
EPS = 1e-6
N_HEADS = 16
KT_ORDER = [3, 4, 2, 5, 1, 6, 0, 7]
PHASE_STARTS = {0: dict(mix=256, ffn=256), 1: dict(mix=384, q=896, ffn=896),
                2: dict(mix=896, ffn=896), 3: dict(mix=896, q=1472, ffn=1408)}


class TL:
    def __init__(self, nc, stack, name, shape, dtype, psum=False):
        if psum:
            self.t = stack.enter_context(nc.psum_tensor(name, shape, dtype))
        else:
            self.t = stack.enter_context(nc.sbuf_tensor(name, shape, dtype))
        self.b = Buf(name, excl=psum)


STATS = {}


class _View:
    def __init__(self, handle, j):
        self.h, self.j = handle, j

    def __getitem__(self, key):
        if not isinstance(key, tuple):
            key = (key,)
        rest = key[1:] if len(key) > 1 else (slice(None),)
        return self.h[(key[0], self.j) + tuple(rest)]


class _View2:
    def __init__(self, handle, b0):
        self.h, self.b0 = handle, b0

    def __getitem__(self, key):
        if not isinstance(key, tuple):
            key = (key,)
        key = tuple(key) + (slice(None),) * (3 - len(key))
        p, hp, cols = key
        if isinstance(hp, slice):
            assert hp == slice(None)
            hp = slice(self.b0, self.b0 + 2)
        else:
            hp = self.b0 + hp
        return self.h[p, hp, cols]


def bank_pair(tl, b0):
    class _O:
        pass
    o = _O()
    o.t = _View2(tl.t, b0)
    return o


class PView:
    def __init__(self, tl, j):
        self.t = _View(tl.t, j)
        self.b = Buf(f"{tl.b.name}_{j}", excl=True)


def build_program(phase_list=None, dbg_x=False):
    nc = bass.Bass("TRN2", target_bir_lowering=False)

    def din(name, shape):
        return nc.dram_tensor(name, list(shape), F32, kind="ExternalInput").ap()

    x_in = din("x", [NCH, 128, WT])
    gmix_d = din("gmix", [128, 4, 8])
    gffn_d = din("gffn", [128, 4, 8])
    poolw_d = din("pool_w", [2, 4, 256, 256])
    poolb_d = din("poolb", [128, 2, 8])
    pools_d = din("pools", [128, 2, 8])
    wqkv_d = din("attn_wqkv", [2, 1024, 3072])
    gq_d = din("gq", [128, 2])
    gk_d = din("gk", [128, 2])
    relb_d = din("relb", [2, 128, 16, 640])
    wo_d = din("attn_wo", [2, 1024, 1024])
    wg_d = din("ffn_w_gate", [4, 1024, FF])
    wv_d = din("ffn_w_val", [4, 1024, FF])
    convw_d = din("convw", [128, 4, NF, 3])
    convb_d = din("convb", [128, 4, NF])
    wout_d = din("ffn_w_out", [4, FF, 1024])
    premask_d = din("premask", [128, 1])
    invcnt_d = din("invcnt", [128, 4, 16])
    y_out = nc.dram_tensor("y", [NCH, 128, OWN], F32, kind="ExternalOutput").ap()
    xs = nc.dram_tensor("xs", [NCH, 128, WT], F32).ap()
    ktd = nc.dram_tensor("ktd", [NCH, 128, WT], BF16).ap()
    qtd = nc.dram_tensor("qtd", [NCH, 128, WT], BF16).ap()
    vad = nc.dram_tensor("vad", [WT, 2048], BF16).ap()
    xdbg = None
    if dbg_x:
        xdbg = nc.dram_tensor("xdbg", [NCH, 128, WT], F32, kind="ExternalOutput").ap()

    with ExitStack() as gst:
        S = Sched(nc, gst)
        PSALL = TL(nc, gst, "psum_all", [128, 8, 512], F32, psum=True)
        P = [PView(PSALL, i) for i in range(8)]
        PP = PSALL

        def G(name, shape, dtype=F32):
            return TL(nc, gst, "g_" + name, shape, dtype)

        ones_d = G("ones_d", [128, 128])
        ones_h = G("ones_h", [128, 128])
        tmp1 = G("tmp1", [128, 128])
        gmix = G("gmix", [128, 4, 8])
        gffn = G("gffn", [128, 4, 8])
        gmixm = G("gmixm", [128, 4, 8])
        gffnm = G("gffnm", [128, 4, 8])
        poolb = G("poolb", [128, 2, 8])
        pools = G("pools", [128, 2, 8])
        poolbs = G("poolbs", [128, 2, 8])
        convw = G("convw", [128, 4, NF, 3])
        convb = G("convb", [128, 4, NF])
        gq = G("gq", [128, 2])
        gk = G("gk", [128, 2])
        premask = G("premask", [128, 1])
        invcnt = G("invcnt", [128, 4, 16])
        cl = S.dsem("d_const")
        for tl, src in ((gmix, gmix_d), (gffn, gffn_d), (poolb, poolb_d), (pools, pools_d),
                        (convw, convw_d), (convb, convb_d), (gq, gq_d), (gk, gk_d),
                        (premask, premask_d), (invcnt, invcnt_d)):
            S.dma("sp", tl.t[:], src, writes=[tl.b], sem=cl)
        for tl in (gmix, gffn, poolb, pools, convw, convb, gq, gk, premask, invcnt):
            tl.b.w = (cl, S.cnt[cl])
        S.I("dve", "memset", writes=[tmp1.b], ap=tmp1.t[:], constant=1.0 / D)
        S.I("dve", "tensor_copy", reads=[tmp1.b], writes=[ones_d.b], out=ones_d.t[:].bitcast(F32R), in_=tmp1.t[:])
        S.I("dve", "memset", writes=[tmp1.b], ap=tmp1.t[:], constant=0.0)
        S.I("dve", "memset", writes=[tmp1.b], ap=tmp1.t[0:64, 0:64], constant=1.0 / 64)
        S.I("dve", "memset", writes=[tmp1.b], ap=tmp1.t[64:128, 64:128], constant=1.0 / 64)
        S.I("dve", "tensor_copy", reads=[tmp1.b], writes=[ones_h.b], out=ones_h.t[:].bitcast(F32R), in_=tmp1.t[:])
        S.I("dve", "tensor_scalar", reads=[gmix.b, premask.b], writes=[gmixm.b], out=gmixm.t[:], in0=gmix.t[:],
            scalar1=premask.t[:, 0:1], scalar2=None, op0=ALU.mult)
        S.I("dve", "tensor_scalar", reads=[gffn.b, premask.b], writes=[gffnm.b], out=gffnm.t[:], in0=gffn.t[:],
            scalar1=premask.t[:, 0:1], scalar2=None, op0=ALU.mult)
        S.I("dve", "tensor_tensor", reads=[poolb.b, pools.b], writes=[poolbs.b], out=poolbs.t[:], in0=poolb.t[:],
            in1=pools.t[:], op=ALU.mult)
        S.I("dve", "tensor_scalar", reads=[gq.b], writes=[gq.b], out=gq.t[:], in0=gq.t[:],
            scalar1=0.125, scalar2=None, op0=ALU.mult)

        C = dict(nc=nc, S=S, P=P, PP=PP, wg_d=wg_d, wv_d=wv_d, wout_d=wout_d, poolw_d=poolw_d, wqkv_d=wqkv_d, ones_d=ones_d, ones_h=ones_h, gmix=gmix, gffn=gffn, gmixm=gmixm,
                 gffnm=gffnm, pools=pools, poolbs=poolbs, convw=convw, convb=convb, gq=gq, gk=gk,
                 premask=premask, invcnt=invcnt)

        with ExitStack() as zst:
            zb = TL(nc, zst, "zero_bf", [128, NCH, 384], BF16)
            zf = TL(nc, zst, "zero_f", [128, NCH, 256], F32)
            S.I("dve", "memset", writes=[zb.b], ap=zb.t[:], constant=0.0)
            S.I("dve", "memset", writes=[zf.b], ap=zf.t[:], constant=0.0)
            zs = S.dsem("d_zero")
            S.dma("sp", ktd[:, :, 0:384].rearrange("c p n -> p c n"), zb.t[:], reads=[zb.b], sem=zs)
            S.dma("sp", qtd[:, :, 0:384].rearrange("c p n -> p c n"), zb.t[:], reads=[zb.b], sem=zs)
            zv = zb.t[:].rearrange("p c n -> p (c n)")
            for i in range(3):
                S.dma("sp", vad[i * 128:(i + 1) * 128, :], zv[:, 0:2048], reads=[zb.b], sem=zs)
            S.dma("sp", xs[:, :, 0:256].rearrange("c p n -> p c n"), zf.t[:], reads=[zf.b], sem=zs)
            S.barrier()

        phases = []
        for l in range(4):
            j = l // 2
            st_ = PHASE_STARTS[l]
            if l % 2 == 0:
                phases.append(("pool", l, j, st_["mix"]))
                phases.append(("ffn", l, st_["ffn"]))
            else:
                phases.append(("qkv", l, j, st_["mix"]))
                phases.append(("attn", l, j, st_["q"]))
                phases.append(("ffn", l, st_["ffn"]))
        full = phase_list is None
        if phase_list is not None:
            phases = phase_list
        src = x_in
        for pi, ph in enumerate(phases):
            last = (pi == len(phases) - 1) and full
            if pi > 0:
                S.barrier()
                S.new_phase()
            if ph[0] == "pool":
                phase_pool(C, ph[1], ph[2], ph[3], src, xs)
                src = xs
            elif ph[0] == "ffn":
                phase_ffn(C, ph[1], ph[2], src, xs, y_out if last else None)
            elif ph[0] == "qkv":
                phase_qkv(C, ph[1], ph[2], ph[3], src, ktd, qtd, vad)
            elif ph[0] == "attn":
                phase_attn(C, ph[1], ph[2], ph[3], src, xs, ktd, qtd, vad, relb_d, wo_d)
        if dbg_x:
            S.barrier()
            with ExitStack() as st:
                cp = TL(nc, st, "dbgcp", [128, 2, 4096], F32)
                ds = S.dsem("d_dbg")
                for c in range(NCH):
                    for hh in range(2):
                        n0 = hh * 4096
                        n1 = min(WT, n0 + 4096)
                        S.dma("sp", cp.t[:, hh, 0:n1 - n0], xs[c, :, n0:n1], writes=[cp.b], sem=ds)
                        S.dma("sp", xdbg[c, :, n0:n1], cp.t[:, hh, 0:n1 - n0], reads=[cp.b], sem=ds)
        S.finish("sp")
        S.finish("pool")
        STATS["nwaits"] = S.nwaits
        STATS["nissue"] = S.nissue
    return nc


def make_tiles(start):
    tiles = []
    t = start
    if t % TS:
        n = TS - t % TS
        tiles.append((t, n))
        t += n
    while t < WT:
        tiles.append((t, TS))
        t += TS
    return tiles


def emit_norm_stats(C, B):
    S, P = C["S"], C["P"]
    xt, sq, rstd = B["xt"], B["sq"], B["rstd"]
    n = B["n"]
    ps = P[0]
    for c in range(NCH):
        s = sq[c % 2]
        S.I("act", "activation", reads=[xt.b], writes=[s.b], out=s.t[:, 0:n].bitcast(F32R), in_=xt.t[:, c, 0:n], func=AF.Square)
        S.I("pe", "matmul", reads=[C["ones_d"].b, s.b], writes=[ps.b],
            out=ps.t[:, 0:n], lhsT=C["ones_d"].t[:].bitcast(F32R), rhs=s.t[:, 0:n].bitcast(F32R), start=(c == 0), stop=(c == NCH - 1))
    S.I("act", "activation", reads=[ps.b], writes=[rstd.b], out=rstd.t[:, 0:n], in_=ps.t[:, 0:n], func=AF.Ln, bias=EPS, scale=1.0)
    S.I("act", "activation", reads=[rstd.b], writes=[rstd.b], out=rstd.t[:, 0:n], in_=rstd.t[:, 0:n], func=AF.Exp, scale=-0.5)


def emit_norm_h(C, B, gt, l):
    S = C["S"]
    xt, rstd, h = B["xt"], B["rstd"], B["h"]
    hoff = B.get("hoff", 0)
    n = B["n"]
    for c in range(NCH):
        S.I("dve", "scalar_tensor_tensor", reads=[xt.b, rstd.b, gt.b], writes=[h.b],
            out=h.t[:, c, hoff:hoff + n], in0=xt.t[:, c, 0:n], scalar=gt.t[:, l, c:c + 1], in1=rstd.t[:, 0:n],
            op0=ALU.mult, op1=ALU.mult)


def emit_norm_b(C, B, t, gt, l):
    emit_norm_stats(C, B)
    emit_norm_h(C, B, gt, l)


def load_x_tile(C, B, tile, src):
    S = C["S"]
    xt = B["xt"]
    tok0, n = tile
    B["n"] = n
    S.dma("sp", xt.t[:, :, 0:n], src[:, :, tok0:tok0 + n].rearrange("c p n -> p c n"), writes=[xt.b],
          sem=S.dsem("d_x" + xt.b.name.split("_")[-1]))


def phase_ffn(C, l, start, src, dst, y_out):
    nc, S, P = C["nc"], C["S"], C["P"]
    OTOK = OT * TS
    with ExitStack() as st:
        def A(name, shape, dtype=F32):
            return TL(nc, st, f"f{l}_{name}", shape, dtype)
        wg = A("wg", [128, NCH, FF], BF16)
        wv = A("wv", [128, NCH, FF], BF16)
        wo = A("wo", [128, NF, D], BF16)
        B = dict(xt=A("xt", [128, NCH, TS]), sq=[A("sq0", [128, TS]), A("sq1", [128, TS])],
                 rstd=A("rstd", [128, TS]), h=A("hT", [128, NCH, TS], BF16))
        u = A("u", [128, NF, TS], BF16)
        asb = [A("asb0", [128, TS + 2]), A("asb1", [128, TS + 2])]
        t1 = [A("t1a", [128, TS]), A("t1b", [128, TS])]
        t2 = [A("t2a", [128, TS]), A("t2b", [128, TS])]
        carry = A("carry", [128, NF, 2])
        xr = [A("xr0", [128, TS]), A("xr1", [128, TS])]
        wgd = C["wg_d"][l].rearrange("(k p) f -> p k f", p=128)
        wvd = C["wv_d"][l].rearrange("(k p) f -> p k f", p=128)
        wod = C["wout_d"][l].rearrange("(j p) d -> p j d", p=128)
        FG = [(0, 6), (6, 12), (12, 17), (17, 22)]
        wg_b = [Buf(f"wg{g}") for g in range(4)]
        wv_b = [Buf(f"wv{g}") for g in range(4)]
        fgrp = {}
        for g, (f0, f1) in enumerate(FG):
            for f in range(f0, f1):
                fgrp[f] = g
            c0, c1 = f0 * 128, f1 * 128
            S.dma("pool", wg.t[:, :, c0:c1], wgd[:, :, c0:c1], writes=[wg_b[g]], sem=S.dsem(f"d_wg{g}"))
            S.dma("pool", wv.t[:, :, c0:c1], wvd[:, :, c0:c1], writes=[wv_b[g]], sem=S.dsem(f"d_wv{g}"))
        for jj in range(0, NF, 2):
            S.dma("pool", wo.t[:, jj:jj + 2, :], wod[:, jj:jj + 2, :], writes=[wo.b], sem=S.dsem("d_wo"))
        carry_b = [Buf() for _ in range(NF)]
        S.I("dve", "memset", writes=[carry.b], ap=carry.t[:], constant=0.0)
        for cbf in carry_b:
            cbf.w = carry.b.w
        cw, cb = C["convw"], C["convb"]

        def gsel(tile):
            return C["gffnm"] if tile[0] < OTOK else C["gffn"]

        tiles = make_tiles(start)
        load_x_tile(C, B, tiles[0], src)
        emit_norm_b(C, B, 0, gsel(tiles[0]), l)
        h = B["h"]
        for ti, (tok0, n) in enumerate(tiles):
            nxt = tiles[ti + 1] if ti + 1 < len(tiles) else None
            for f in range(NF):
                pg, pv = P[1 + f % 2], P[3 + f % 2]
                for k in range(NCH):
                    S.I("pe", "matmul", reads=[wg_b[fgrp[f]], h.b], writes=[pg.b], signal=(k == NCH - 1),
                        out=pg.t[:, 0:n], lhsT=wg.t[:, k, f * 128:(f + 1) * 128], rhs=h.t[:, k, 0:n], start=(k == 0), stop=(k == NCH - 1))
                for k in range(NCH):
                    S.I("pe", "matmul", reads=[wv_b[fgrp[f]], h.b], writes=[pv.b], signal=(k == NCH - 1),
                        out=pv.t[:, 0:n], lhsT=wv.t[:, k, f * 128:(f + 1) * 128], rhs=h.t[:, k, 0:n], start=(k == 0), stop=(k == NCH - 1))
                a = asb[f % 2]
                ta, tb = t1[f % 2], t2[f % 2]
                S.I("act", "activation", reads=[pg.b], writes=[a.b], out=a.t[:, 2:n + 2], in_=pg.t[:, 0:n], func=AF.Copy)
                S.I("act", "activation", reads=[pg.b, cw.b, cb.b], writes=[ta.b], out=ta.t[:, 0:n], in_=pg.t[:, 0:n], func=AF.Identity,
                    bias=cb.t[:, l, f:f + 1], scale=cw.t[:, l, f, 2:3])
                S.I("dve", "tensor_copy", reads=[carry_b[f]], writes=[a.b], out=a.t[:, 0:2], in_=carry.t[:, f, :])
                S.I("dve", "tensor_copy", reads=[a.b], writes=[carry_b[f]], out=carry.t[:, f, :], in_=a.t[:, n:n + 2])
                S.I("dve", "scalar_tensor_tensor", reads=[a.b, ta.b, cw.b], writes=[tb.b], out=tb.t[:, 0:n], in0=a.t[:, 0:n],
                    scalar=cw.t[:, l, f, 0:1], in1=ta.t[:, 0:n], op0=ALU.mult, op1=ALU.add)
                S.I("dve", "scalar_tensor_tensor", reads=[a.b, tb.b, cw.b], writes=[ta.b], out=ta.t[:, 0:n], in0=a.t[:, 1:n + 1],
                    scalar=cw.t[:, l, f, 1:2], in1=tb.t[:, 0:n], op0=ALU.mult, op1=ALU.add)
                S.I("act", "activation", reads=[ta.b], writes=[tb.b], out=tb.t[:, 0:n], in_=ta.t[:, 0:n], func=AF.Silu)
                S.I("dve", "tensor_tensor", reads=[tb.b, pv.b], writes=[u.b], out=u.t[:, f, 0:n], in0=tb.t[:, 0:n], in1=pv.t[:, 0:n], op=ALU.mult)
            if nxt is not None:
                load_x_tile(C, B, nxt, src)
            for dch in range(NCH):
                if dch == 4 and nxt is not None:
                    emit_norm_b(C, B, 0, gsel(nxt), l)
                po = P[5 + dch % 2]
                x_r = xr[dch % 2]
                S.dma("sp", x_r.t[:, 0:n], src[dch, :, tok0:tok0 + n], writes=[x_r.b], sem=S.dsem(f"d_xr{dch % 2}"))
                for f in range(NF):
                    S.I("pe", "matmul", reads=[wo.b, u.b], writes=[po.b], signal=(f == NF - 1),
                        out=po.t[:, 0:n], lhsT=wo.t[:, f, dch * 128:(dch + 1) * 128], rhs=u.t[:, f, 0:n], start=(f == 0), stop=(f == NF - 1))
                S.I("dve", "tensor_tensor", reads=[x_r.b, po.b], writes=[x_r.b], out=x_r.t[:, 0:n], in0=x_r.t[:, 0:n], in1=po.t[:, 0:n], op=ALU.add)
                if y_out is None:
                    S.dma("pool", dst[dch, :, tok0:tok0 + n], x_r.t[:, 0:n], reads=[x_r.b], sem=S.dsem(f"d_sx{dch % 2}"))
                elif tok0 >= OTOK:
                    S.dma("pool", y_out[dch, :, tok0 - OTOK:tok0 - OTOK + n], x_r.t[:, 0:n], reads=[x_r.b], sem=S.dsem(f"d_sx{dch % 2}"))


def phase_pool(C, l, j, start, src, dst):
    nc, S, P = C["nc"], C["S"], C["P"]
    HX = 16
    OTOK = OT * TS
    with ExitStack() as st:
        def A(name, shape, dtype=F32):
            return TL(nc, st, f"p{l}_{name}", shape, dtype)
        pw = A("pw", [128, NCH, 256], BF16)
        xts = [A("xt0", [128, NCH, TS]), A("xt1", [128, NCH, TS])]
        B = dict(xt=xts[0], sq=[A("sq0", [128, TS]), A("sq1", [128, TS])],
                 rstd=A("rstd", [128, TS]), h=A("hx", [128, NCH, HX + TS]), hoff=HX)
        sA = A("sA", [128, NCH, HX + TS])
        sB = A("sB", [128, NCH, HX + TS])
        dT = A("dT", [128, NCH, TS], BF16)
        NOST = 6
        ost = [A(f"ost{i}", [128, TS]) for i in range(NOST)]
        tfix = A("tfix", [128, 16])
        pwd = C["poolw_d"][j].rearrange("g (jj p) d -> p g jj d", p=128)
        for g in range(4):
            S.dma("pool", pw.t[:, 2 * g:2 * g + 2, :], pwd[:, g, :, :], writes=[pw.b], sem=S.dsem("d_pw"))
        hx = B["h"]
        S.I("dve", "memset", writes=[hx.b], ap=hx.t[:, :, 0:HX], constant=0.0)

        def gsel(tile):
            return C["gmixm"] if tile[0] < OTOK else C["gmix"]

        tiles = make_tiles(start)
        B["xt"] = xts[0]
        load_x_tile(C, B, tiles[0], src)
        emit_norm_stats(C, B)
        n_prev = None
        for ti, (tok0, n) in enumerate(tiles):
            nxt = tiles[ti + 1] if ti + 1 < len(tiles) else None
            NW = HX + n
            B["xt"] = xts[ti % 2]
            B["n"] = n
            if ti > 0:
                S.I("dve", "tensor_copy", reads=[hx.b], writes=[hx.b], out=hx.t[:, :, 0:HX], in_=hx.t[:, :, n_prev:n_prev + HX])
            emit_norm_h(C, B, gsel((tok0, n)), l)
            if nxt is not None:
                B["xt"] = xts[(ti + 1) % 2]
                load_x_tile(C, B, nxt, src)
                emit_norm_stats(C, B)
                B["xt"] = xts[ti % 2]
                B["n"] = n
            S.I("dve", "tensor_tensor", reads=[hx.b], writes=[sA.b], out=sA.t[:, :, 1:NW], in0=hx.t[:, :, 1:NW], in1=hx.t[:, :, 0:NW - 1], op=ALU.add)
            S.I("dve", "tensor_tensor", reads=[sA.b], writes=[sB.b], out=sB.t[:, 2:8, 3:NW], in0=sA.t[:, 2:8, 3:NW], in1=sA.t[:, 2:8, 1:NW - 2], op=ALU.add)
            S.I("dve", "tensor_tensor", reads=[sB.b], writes=[sA.b], out=sA.t[:, 4:8, 7:NW], in0=sB.t[:, 4:8, 7:NW], in1=sB.t[:, 4:8, 3:NW - 4], op=ALU.add)
            S.I("dve", "tensor_tensor", reads=[sA.b], writes=[sB.b], out=sB.t[:, 6:8, 15:NW], in0=sA.t[:, 6:8, 15:NW], in1=sA.t[:, 6:8, 7:NW - 8], op=ALU.add)
            srcs = [sA, sB, sA, sB]
            for g in range(4):
                w = 2 << g
                sg = srcs[g]
                S.I("dve", "scalar_tensor_tensor", reads=[sg.b, hx.b], writes=[dT.b], out=dT.t[:, 2 * g:2 * g + 2, 0:n],
                    in0=sg.t[:, 2 * g:2 * g + 2, HX:NW], scalar=1.0 / w, in1=hx.t[:, 2 * g:2 * g + 2, HX:NW],
                    op0=ALU.mult, op1=ALU.subtract)
            if tok0 == OTOK:
                ic = C["invcnt"]
                for g in range(4):
                    sg = srcs[g]
                    for c in (2 * g, 2 * g + 1):
                        S.I("dve", "tensor_tensor", reads=[sg.b, ic.b], writes=[tfix.b], out=tfix.t[:], in0=sg.t[:, c, HX:HX + 16],
                            in1=ic.t[:, g, :], op=ALU.mult)
                        S.I("dve", "tensor_tensor", reads=[tfix.b, hx.b], writes=[dT.b], out=dT.t[:, c, 0:16], in0=tfix.t[:],
                            in1=hx.t[:, c, HX:HX + 16], op=ALU.subtract)
            xt = B["xt"]
            for g in range(4):
                for jo in range(2):
                    co = 2 * g + jo
                    po = P[1 + co % 4]
                    for ji in range(2):
                        S.I("pe", "matmul", reads=[pw.b, dT.b], writes=[po.b], signal=(ji == 1), out=po.t[:, 0:n],
                            lhsT=pw.t[:, 2 * g + ji, jo * 128:(jo + 1) * 128], rhs=dT.t[:, 2 * g + ji, 0:n], start=(ji == 0), stop=(ji == 1))
                    o = ost[(ti * 8 + co) % NOST]
                    S.I("act", "activation", reads=[po.b, C["pools"].b, C["poolbs"].b], writes=[o.b], out=o.t[:, 0:n], in_=po.t[:, 0:n],
                        func=AF.Identity, bias=C["poolbs"].t[:, j, co:co + 1], scale=C["pools"].t[:, j, co:co + 1])
                    S.I("dve", "tensor_tensor", reads=[o.b, xt.b], writes=[o.b], out=o.t[:, 0:n], in0=o.t[:, 0:n], in1=xt.t[:, co, 0:n], op=ALU.add)
                    S.dma("pool", dst[co, :, tok0:tok0 + n], o.t[:, 0:n], reads=[o.b], sem=S.dsem(f"d_so{(ti * 8 + co) % NOST}"))
            n_prev = n


def phase_qkv(C, l, j, start, src, ktd, qtd, vad):
    nc, S, P = C["nc"], C["S"], C["P"]
    with ExitStack() as st:
        def A(name, shape, dtype=F32):
            return TL(nc, st, f"q{l}_{name}", shape, dtype)
        wq = A("wqkv", [128, NCH, 3 * D], BF16)
        B = dict(xt=A("xt", [128, NCH, TS]), sq=[A("sq0", [128, TS]), A("sq1", [128, TS])],
                 rstd=A("rstd", [128, TS]), h=None)
        hTs = [A("hT0", [128, NCH, TS], BF16), A("hT1", [128, NCH, TS], BF16)]
        sqh = [A("sqh0", [128, TS]), A("sqh1", [128, TS])]
        rh = [A("rh0", [128, TS]), A("rh1", [128, TS])]
        qo = [A(f"qo{i}", [128, TS], BF16) for i in range(3)]
        vo = [A(f"vo{i}", [128, 8, 2, 128], BF16) for i in range(2)]
        vop = [A(f"vop{i}", [128, 8, 2, 128], BF16) for i in range(2)]
        wqd = C["wqkv_d"][j].rearrange("(k p) f -> p k f", p=128)
        wq_b = [Buf(f"wq{g}") for g in range(6)]
        for g in range(6):
            S.dma("pool", wq.t[:, :, g * 512:(g + 1) * 512], wqd[:, :, g * 512:(g + 1) * 512], writes=[wq_b[g]], sem=S.dsem(f"d_wq{g}"))
        pm = C["premask"]
        for v in vo + vop:
            S.I("dve", "memset", writes=[v.b], ap=v.t[:], constant=1.0)
        for v in vop:
            S.I("dve", "tensor_scalar", reads=[v.b, pm.b], writes=[v.b], out=v.t[:], in0=v.t[:], scalar1=pm.t[:, 0:1],
                scalar2=None, op0=ALU.mult)

        OTOK = OT * TS

        def gsel(tile):
            return C["gmixm"] if tile[0] < OTOK else C["gmix"]

        tiles = make_tiles(start)
        B["h"] = hTs[0]
        load_x_tile(C, B, tiles[0], src)
        emit_norm_b(C, B, 0, gsel(tiles[0]), l)
        for ti, (tok0, n) in enumerate(tiles):
            nxt = tiles[ti + 1] if ti + 1 < len(tiles) else None
            h = hTs[ti % 2]
            cols = slice(tok0, tok0 + n)
            pend = None

            def finish(pend):
                m, pq, s_ = pend
                ph = P[4 + m % 2]
                r_ = rh[m % 2]
                S.I("pe", "matmul", reads=[C["ones_h"].b, s_.b], writes=[ph.b], out=ph.t[:, 0:n], lhsT=C["ones_h"].t[:].bitcast(F32R),
                    rhs=s_.t[:, 0:n].bitcast(F32R), start=True, stop=True)
                S.I("act", "activation", reads=[ph.b], writes=[r_.b], out=r_.t[:, 0:n], in_=ph.t[:, 0:n], func=AF.Ln, bias=EPS, scale=1.0)
                S.I("act", "activation", reads=[r_.b], writes=[r_.b], out=r_.t[:, 0:n], in_=r_.t[:, 0:n], func=AF.Exp, scale=-0.5)
                o = qo[m % 3]
                gsc = C["gq"] if m < 8 else C["gk"]
                S.I("dve", "scalar_tensor_tensor", reads=[pq.b, r_.b, gsc.b], writes=[o.b], out=o.t[:, 0:n], in0=pq.t[:, 0:n],
                    scalar=gsc.t[:, j:j + 1], in1=r_.t[:, 0:n], op0=ALU.mult, op1=ALU.mult)
                dstd = qtd if m < 8 else ktd
                S.dma("pool", dstd[m % 8, :, cols], o.t[:, 0:n], reads=[o.b], sem=S.dsem(f"d_qo{m % 3}"))

            for m in range(16):
                pq = P[1 + m % 3]
                for k in range(NCH):
                    S.I("pe", "matmul", reads=[wq_b[m // 4], h.b], writes=[pq.b], signal=(k == NCH - 1), out=pq.t[:, 0:n],
                        lhsT=wq.t[:, k, m * 128:(m + 1) * 128], rhs=h.t[:, k, 0:n], start=(k == 0), stop=(k == NCH - 1))
                s_ = sqh[m % 2]
                S.I("act", "activation", reads=[pq.b], writes=[s_.b], out=s_.t[:, 0:n].bitcast(F32R), in_=pq.t[:, 0:n], func=AF.Square)
                if pend is not None:
                    finish(pend)
                pend = (m, pq, s_)
            if nxt is not None:
                B["h"] = hTs[(ti + 1) % 2]
                load_x_tile(C, B, nxt, src)
                emit_norm_b(C, B, 0, gsel(nxt), l)
            vbufs = vop if tok0 < OTOK else vo
            for tb in range(n // 128):
                v = vbufs[tb % 2]
                for cb in range(2):
                    pvv = P[6 + cb]
                    for k in range(NCH):
                        S.I("pe", "matmul", reads=[wq_b[4 + cb], h.b], writes=[pvv.b], signal=(k == NCH - 1), out=pvv.t[:],
                            lhsT=h.t[:, k, tb * 128:(tb + 1) * 128], rhs=wq.t[:, k, 2 * D + cb * 512:2 * D + (cb + 1) * 512],
                            start=(k == 0), stop=(k == NCH - 1))
                    if pend is not None:
                        finish(pend)
                        pend = None
                    pr = pvv.t[:].rearrange("p (i e d) -> p i e d", e=2, d=64)
                    S.I("act", "activation", reads=[pvv.b], writes=[v.b], out=v.t[:, cb * 4:(cb + 1) * 4, 0, 0:64], in_=pr[:, :, 0, :], func=AF.Copy)
                    S.I("act", "activation", reads=[pvv.b], writes=[v.b], out=v.t[:, cb * 4:(cb + 1) * 4, 1, 64:128], in_=pr[:, :, 1, :], func=AF.Copy)
                r0 = tok0 + tb * 128
                S.dma("pool", vad[r0:r0 + 128, :], v.t[:].rearrange("p a b c -> p (a b c)"), reads=[v.b],
                      sem=S.dsem(f"d_vo{tb % 2}{'p' if tok0 < OTOK else ''}"))


def phase_attn(C, l, j, q_tok, src, dst, ktd, qtd, vad, relb_d, wo_d):
    nc, S, P, PP = C["nc"], C["S"], C["P"], C["PP"]
    NPAIR = N_HEADS // 2
    tq = q_tok // TS
    cmin0 = (q_tok % TS) // 64
    with ExitStack() as st:
        def A(name, shape, dtype=F32):
            return TL(nc, st, f"a{l}_{name}", shape, dtype)
        E = A("E", [128, N_HEADS, 640])
        wo = A("wo", [128, NCH, D], BF16)
        Q = [A(f"Q{i}", [128, NCH, TS], BF16) for i in range(2)]
        Kt = [A(f"K{i}", [128, NCH, TS], BF16) for i in range(3)]
        Vt = [A(f"V{i}", [128, 4, 2048], BF16) for i in range(3)]
        NS = 4
        bank_ptr = [0]
        pexp = [A(f"pexp{i}", [128, 2, TS]) for i in range(NS)]
        pT = [A(f"pT{i}", [128, 2, TS], BF16) for i in range(NS + 1)]
        rden = [A(f"rden{i}", [128, 2, TS]) for i in range(2)]
        accs = [A(f"accs{i}", [128, 2, TS]) for i in range(2)]
        oT = A("oT", [128, NCH, TS], BF16)
        NXR = 4
        xr = [A(f"xr{i}", [128, TS]) for i in range(NXR)]
        E_b = [Buf(f"E{g}") for g in range(4)]
        for g in range(4):
            hh = 4 * g
            S.dma("sp", E.t[:, hh:hh + 4, :], relb_d[j, :, hh:hh + 4, :], writes=[E_b[g]], sem=S.dsem(f"d_E{g}"))
            S.I("act", "activation", reads=[E_b[g]], writes=[E_b[g]], out=E.t[:, hh:hh + 4, :], in_=E.t[:, hh:hh + 4, :], func=AF.Exp)
        wod = wo_d[j].rearrange("(k p) d -> p k d", p=128)
        for k in range(0, NCH, 2):
            S.dma("pool", wo.t[:, k:k + 2, :], wod[:, k:k + 2, :], writes=[wo.b], sem=S.dsem("d_awo"))

        def load_kv(t):
            kt_, vt_ = Kt[t % 3], Vt[t % 3]
            S.dma("sp", kt_.t[:], ktd[:, :, t * TS:(t + 1) * TS].rearrange("c p n -> p c n"), writes=[kt_.b], sem=S.dsem(f"d_K{t % 3}"))
            S.dma("sp", vt_.t[:], vad[t * TS:(t + 1) * TS, :].rearrange("(kt p) f -> p kt f", p=128), writes=[vt_.b], sem=S.dsem(f"d_V{t % 3}"))

        def load_q(t):
            q_ = Q[t % 2]
            S.dma("sp", q_.t[:], qtd[:, :, t * TS:(t + 1) * TS].rearrange("c p n -> p c n"), writes=[q_.b], sem=S.dsem(f"d_Q{t % 2}"))

        load_kv(tq - 1)
        load_kv(tq)
        load_q(tq)
        step = 0
        for bt in range(tq, NT):
            if bt + 1 < NT:
                load_kv(bt + 1)
                load_q(bt + 1)
            q_ = Q[bt % 2]
            pending = []
            norm_q = []
            cmin = cmin0 if bt == tq else 0
            cs = cmin * 64
            kts = []
            for jk in KT_ORDER:
                qc0 = max(0, 2 * jk - 8, cmin)
                qc1 = min(8, 2 * jk + 2)
                if qc1 > qc0:
                    kts.append((jk, qc0, qc1))
            nk = len(kts)

            def do_evac(c):
                ac = accs[c % 2]
                S.I("act", "activation", reads=[P[6].b, P[7].b], writes=[ac.b], out=ac.t[:, :, cs:TS], in_=bank_pair(PP, 6).t[:, :, cs:TS], func=AF.Copy)

            def norm_tasks(c):
                ac = accs[c % 2]
                rd = rden[c % 2]

                def ln(hp):
                    dr = slice((1 - hp) * 64, (1 - hp) * 64 + 64)
                    S.I("act", "activation", reads=[ac.b], writes=[rd.b], out=rd.t[dr, hp, cs:TS], in_=ac.t[dr, hp, cs:TS], func=AF.Ln, bias=1e-30, scale=1.0)

                def fin(hp):
                    nr = slice(hp * 64, hp * 64 + 64)
                    dr = slice((1 - hp) * 64, (1 - hp) * 64 + 64)
                    S.I("act", "activation", reads=[rd.b], writes=[rd.b], out=rd.t[nr, hp, cs:TS], in_=rd.t[dr, hp, cs:TS], func=AF.Exp, scale=-1.0)
                    S.I("dve", "tensor_tensor", reads=[ac.b, rd.b], writes=[oT.b], out=oT.t[nr, c, cs:TS], in0=ac.t[nr, hp, cs:TS], in1=rd.t[nr, hp, cs:TS], op=ALU.mult)

                return [lambda: ln(0), lambda: ln(1), lambda: fin(0), lambda: fin(1)]

            def do_pv(item):
                (c, idx, jk, n, qs, pt) = item
                last = (idx == nk - 1)
                acc = bank_pair(PP, 6)
                accb = [P[6].b, P[7].b]
                tsrc = bt - 1 if jk < 4 else bt
                vt_ = Vt[tsrc % 3]
                for hp in range(2):
                    h = 2 * c + hp
                    S.I("pe", "matmul", reads=[vt_.b, pt.b], writes=accb, signal=(last and hp == 1), out=acc.t[:, hp, qs:qs + n],
                        lhsT=vt_.t[:, jk % 4, h * 128:(h + 1) * 128], rhs=pt.t[:, hp, 0:n], start=(idx == 0), stop=last,
                        skip_group_check=True)
                if last:
                    do_evac(c)
                    norm_q.extend(norm_tasks(c))

            for c in range(NPAIR):
                for idx, (jk, qc0, qc1) in enumerate(kts):
                    n = (qc1 - qc0) * 64
                    qs = qc0 * 64
                    c0 = qs + 512 - 128 * jk
                    tsrc = bt - 1 if jk < 4 else bt
                    kt_ = Kt[tsrc % 3]
                    kc = (jk % 4) * 128
                    nb = 1 if n <= 256 else 2
                    if nb == 2 and bank_ptr[0] == 5:
                        bank_ptr[0] = 0
                    b0 = bank_ptr[0]
                    bank_ptr[0] = (b0 + nb) % 6
                    psb = [P[b0 + i].b for i in range(nb)]
                    if nb == 2:
                        ps_h = [P[b0].t[:, 0:n], P[b0 + 1].t[:, 0:n]]
                        ps_all = bank_pair(PP, b0).t[:, :, 0:n]
                    else:
                        ps_h = [P[b0].t[:, 0:n], P[b0].t[:, 256:256 + n]]
                        ps_all = P[b0].t[:].rearrange("p (h n) -> p h n", h=2)[:, :, 0:n]
                    for hp in range(2):
                        pr = slice(hp * 64, hp * 64 + 64)
                        if nb == 1 and hp == 1:
                            S.wait("pe", [psb[0].w])
                        S.I("pe", "matmul", reads=[kt_.b, q_.b], writes=psb, signal=(hp == 1 or nb == 1), out=ps_h[hp], lhsT=kt_.t[pr, c, kc:kc + 128],
                            rhs=q_.t[pr, c, qs:qs + n], start=True, stop=True)
                    pe_ = pexp[step % NS]
                    pt = pT[step % (NS + 1)]
                    S.I("act", "activation", reads=psb, writes=[pe_.b], out=pe_.t[:, :, 0:n], in_=ps_all, func=AF.Exp)
                    S.I("dve", "tensor_tensor", reads=[pe_.b, E_b[c // 2]], writes=[pt.b], out=pt.t[:, :, 0:n], in0=pe_.t[:, :, 0:n],
                        in1=E.t[:, 2 * c:2 * c + 2, c0:c0 + n], op=ALU.mult)
                    pending.append((c, idx, jk, n, qs, pt))
                    if len(pending) > NS - 1:
                        do_pv(pending.pop(0))
                    if norm_q:
                        norm_q.pop(0)()
                    step += 1
            while pending:
                do_pv(pending.pop(0))
            while norm_q:
                norm_q.pop(0)()
            for dch in range(NCH):
                pw_ = P[dch % 2]
                x_r = xr[dch % NXR]
                S.dma("sp", x_r.t[:, cs:TS], src[dch, :, bt * TS + cs:(bt + 1) * TS], writes=[x_r.b], sem=S.dsem(f"d_xr{dch % NXR}"))
                for k in range(NCH):
                    S.I("pe", "matmul", reads=[wo.b, oT.b], writes=[pw_.b], signal=(k == NCH - 1), out=pw_.t[:, cs:TS],
                        lhsT=wo.t[:, k, dch * 128:(dch + 1) * 128], rhs=oT.t[:, k, cs:TS], start=(k == 0), stop=(k == NCH - 1))
                S.I("dve", "tensor_tensor", reads=[x_r.b, pw_.b], writes=[x_r.b], out=x_r.t[:, cs:TS], in0=x_r.t[:, cs:TS], in1=pw_.t[:, cs:TS], op=ALU.add)
                S.dma("pool", dst[dch, :, bt * TS + cs:(bt + 1) * TS], x_r.t[:, cs:TS], reads=[x_r.b], sem=S.dsem(f"d_sx{dch % NXR}"))


HALO = OT * TS


def _chunked(v):
    v = np.asarray(v, np.float32)
    lead = v.shape[:-1]
    n = v.shape[-1] // 128
    v = v.reshape(lead + (n, 128))
    return np.ascontiguousarray(np.moveaxis(v, -1, 0))


def prepare_inputs(x, mix_norm, ffn_norm, pool_w, pool_b, pool_scale, attn_wqkv, attn_q_norm,
                   attn_k_norm, attn_rel_bias, attn_wo, ffn_w_gate, ffn_w_val, ffn_conv_w,
                   ffn_conv_b, ffn_w_out):
    f32 = np.float32
    x = np.asarray(x, f32)
    shared = {
        "gmix": _chunked(mix_norm), "gffn": _chunked(ffn_norm),
        "pool_w": np.ascontiguousarray(pool_w, f32),
        "poolb": _chunked(pool_b), "pools": _chunked(pool_scale),
        "attn_wqkv": np.ascontiguousarray(attn_wqkv, f32),
        "attn_wo": np.ascontiguousarray(attn_wo, f32),
        "ffn_w_gate": np.ascontiguousarray(ffn_w_gate, f32),
        "ffn_w_val": np.ascontiguousarray(ffn_w_val, f32),
        "ffn_w_out": np.ascontiguousarray(ffn_w_out, f32),
    }
    cw = np.asarray(ffn_conv_w, f32).reshape(4, 3, NF, 128)
    shared["convw"] = np.ascontiguousarray(cw.transpose(3, 0, 2, 1))
    shared["convb"] = _chunked(ffn_conv_b)
    gq = np.asarray(attn_q_norm, f32)
    gk = np.asarray(attn_k_norm, f32)
    shared["gq"] = np.ascontiguousarray(np.concatenate([gq, gq], axis=1).T)
    shared["gk"] = np.ascontiguousarray(np.concatenate([gk, gk], axis=1).T)
    p = np.arange(128)[:, None]
    c = np.arange(640)[None, :]
    idx = np.clip(c - p, -256, 256) + 256
    dchunk = c // 64 - p // 64
    valid = (dchunk >= 0) & (dchunk <= 8)
    rb = np.asarray(attn_rel_bias, f32)
    g = rb[:, :, idx]
    g = np.where(valid[None, None], g, f32(-30000.0))
    shared["relb"] = np.ascontiguousarray(g.transpose(0, 2, 1, 3))
    in_maps = []
    B, Sq, _ = x.shape
    for core in range(8):
        b, half = core // 2, core % 2
        s0 = half * OWN
        xw = np.zeros((WT, D), f32)
        lo = s0 - HALO
        if lo >= 0:
            xw[:] = x[b, lo:s0 + OWN]
        else:
            xw[HALO:] = x[b, 0:OWN]
        m = dict(shared)
        m["x"] = np.ascontiguousarray(xw.T.reshape(NCH, 128, WT))
        m["premask"] = np.full((128, 1), 1.0 if half else 0.0, f32)
        ic = np.zeros((128, 4, 16), f32)
        for gi, w in enumerate((2, 4, 8, 16)):
            if half:
                ic[:, gi, :] = 1.0 / w
            else:
                ic[:, gi, :] = 1.0 / np.minimum(np.arange(16) + 1, w)
        m["invcnt"] = ic
        in_maps.append(m)
    return in_maps


_NC_CACHE = {}


def kernel(**inputs):
    in_maps = prepare_inputs(**inputs)
    if "full" not in _NC_CACHE:
        _NC_CACHE["full"] = build_program()
    nc = _NC_CACHE["full"]
    res = run_bass_kernel_spmd(nc, in_maps, core_ids=list(range(8)))
    x = inputs["x"]
    out = np.empty(x.shape, np.float32)
    for core in range(8):
        b, half = core // 2, core % 2
        y = res.results[core]["y"].reshape(D, OWN)
        out[b, half * OWN:(half + 1) * OWN, :] = y.T
    return out
```

```python
import numpy as np
from contextlib import ExitStack
import concourse.bass as bass
import concourse.mybir as mybir
from concourse.bass_utils import run_bass_kernel_spmd

F32 = mybir.dt.float32
F32R = mybir.dt.float32r
BF16 = mybir.dt.bfloat16
AF = mybir.ActivationFunctionType
ALU = mybir.AluOpType


class Buf:
    __slots__ = ("name", "w", "r", "excl")

    def __init__(self, name="", excl=False):
        self.name = name
        self.excl = excl
        self.w = None
        self.r = {}


class Sched:
    ENGS = ("pe", "act", "dve", "pool", "sp")

    def __init__(self, nc, stack):
        self.nc = nc
        self.stack = stack
        self.eng = {"pe": nc.tensor, "act": nc.scalar, "dve": nc.vector,
                    "pool": nc.gpsimd, "sp": nc.sync}
        self.sems = {}
        self.cnt = {}
        self.seen = {e: {} for e in self.ENGS}
        self.vc = {}
        self.snap = {}
        self.order = {}
        self.nissue = 0
        self.nwaits = 0
        self.ekey = {}
        self.nphase = 0
        self.new_phase()

    def new_sem(self, key):
        self.sems[key] = self.stack.enter_context(self.nc.semaphore(key))
        self.cnt[key] = 0
        return key

    def new_phase(self):
        p = self.nphase
        self.nphase += 1
        for e in self.ENGS:
            if e in ("sp", "pool"):
                self.ekey[e] = None
                continue
            self.ekey[e] = self.new_sem(f"e_{e}_{p}")

    def _need(self, e, evs):
        seen = self.seen[e]
        need = {}
        for ev in evs:
            if ev is None:
                continue
            k, v = ev
            if seen.get(k, 0) >= v:
                continue
            if need.get(k, 0) < v:
                need[k] = v
        out = []
        for k, v in sorted(need.items(), key=lambda kv: -self.order.get(kv, 0)):
            if seen.get(k, 0) >= v:
                continue
            out.append((k, v))
            self._learn(e, k, v)
        return out

    def _learn(self, e, k, v):
        seen = self.seen[e]
        if seen.get(k, 0) < v:
            seen[k] = v
        vc = self.vc.get((k, v))
        if vc is not None:
            for kk, vv in vc.items():
                if seen.get(kk, 0) < vv:
                    seen[kk] = vv
        self.snap[e] = None

    def _snapshot(self, e):
        sn = self.snap.get(e)
        if sn is None:
            sn = dict(self.seen[e])
            self.snap[e] = sn
        return sn

    def wait(self, e, evs, keep_one=False):
        need = self._need(e, evs)
        attach = None
        if keep_one and need:
            attach = need.pop()
        for k, v in need:
            assert v <= self.cnt[k] + 1, (k, v, self.cnt[k])
            self.eng[e].wait_ge(self.sems[k], v)
            self.nwaits += 1
        return attach

    def _deps(self, e, reads, writes, is_dma):
        own = self.ekey[e] if (e == "pe" and not is_dma) else None
        evs = []
        for b in reads:
            evs.append(b.w)
            if b.excl:
                for k, v in b.r.items():
                    if k != own:
                        evs.append((k, v))
        for b in writes:
            if b.w is not None and b.w[0] != own:
                evs.append(b.w)
            for k, v in b.r.items():
                if k != own:
                    evs.append((k, v))
        return evs

    def _commit(self, ev, reads, writes):
        k, v = ev
        for b in reads:
            if b.r.get(k, 0) < v:
                b.r[k] = v
        for b in writes:
            b.w = ev
            b.r = {}

    def _issued(self, e, ev):
        self.nissue += 1
        self.order[ev] = self.nissue
        self.vc[ev] = self._snapshot(e)

    def op(self, e, fn, reads=(), writes=(), signal=True):
        att = self.wait(e, self._deps(e, reads, writes, False), keep_one=True)
        ins = fn(self.eng[e])
        if att is not None:
            ins._wait_ge(self.sems[att[0]], att[1])
        k = self.ekey[e]
        if signal:
            ins.then_inc(self.sems[k], 1)
            self.cnt[k] += 1
            ev = (k, self.cnt[k])
        else:
            ev = (k, self.cnt[k] + 1)
        self._issued(e, ev)
        self._commit(ev, reads, writes)
        return ins

    def dma(self, q, out, in_, reads=(), writes=(), sem=None):
        self.wait(q, self._deps(q, reads, writes, True))
        ins = self.eng[q].dma_start(out=out, in_=in_)
        ins.then_inc(self.sems[sem], 16)
        self.cnt[sem] += 16
        self._issued(q, (sem, self.cnt[sem]))
        self._commit((sem, self.cnt[sem]), reads, writes)
        return ins

    def all_events(self):
        return [(k, v) for k, v in self.cnt.items() if v > 0]

    def barrier(self):
        evs = self.all_events()
        for e in self.ENGS:
            self.wait(e, evs)

    def finish(self, e="sp"):
        self.wait(e, self.all_events())

    def I(self, e, meth, reads=(), writes=(), signal=True, **kw):
        att = self.wait(e, self._deps(e, reads, writes, False), keep_one=True)
        ins = getattr(self.eng[e], meth)(**kw)
        if att is not None:
            ins._wait_ge(self.sems[att[0]], att[1])
        k = self.ekey[e]
        if signal:
            ins.then_inc(self.sems[k], 1)
            self.cnt[k] += 1
            ev = (k, self.cnt[k])
        else:
            ev = (k, self.cnt[k] + 1)
        self._issued(e, ev)
        self._commit(ev, reads, writes)
        return ins

    def dsem(self, name):
        if name not in self.sems:
            self.new_sem(name)
        return name


D = 1024
NCH = 8
FF = 2816
NF = 22
TS = 512
WT = 5632
NT = WT // TS
OT = 3
OWN = 4096
EPS = 1e-6
N_HEADS = 16
KT_ORDER = [3, 4, 2, 5, 1, 6, 0, 7]
PHASE_STARTS = {0: dict(mix=256, ffn=256), 1: dict(mix=384, q=896, ffn=896),
                2: dict(mix=896, ffn=896), 3: dict(mix=896, q=1472, ffn=1408)}


class TL:
    def __init__(self, nc, stack, name, shape, dtype, psum=False):
        if psum:
            self.t = stack.enter_context(nc.psum_tensor(name, shape, dtype))
        else:
            self.t = stack.enter_context(nc.sbuf_tensor(name, shape, dtype))
        self.b = Buf(name, excl=psum)


STATS = {}


class _View:
    def __init__(self, handle, j):
        self.h, self.j = handle, j

    def __getitem__(self, key):
        if not isinstance(key, tuple):
            key = (key,)
        rest = key[1:] if len(key) > 1 else (slice(None),)
        return self.h[(key[0], self.j) + tuple(rest)]


class _View2:
    def __init__(self, handle, b0):
        self.h, self.b0 = handle, b0

    def __getitem__(self, key):
        if not isinstance(key, tuple):
            key = (key,)
        key = tuple(key) + (slice(None),) * (3 - len(key))
        p, hp, cols = key
        if isinstance(hp, slice):
            assert hp == slice(None)
            hp = slice(self.b0, self.b0 + 2)
        else:
            hp = self.b0 + hp
        return self.h[p, hp, cols]


def bank_pair(tl, b0):
    class _O:
        pass
    o = _O()
    o.t = _View2(tl.t, b0)
    return o


class PView:
    def __init__(self, tl, j):
        self.t = _View(tl.t, j)
        self.b = Buf(f"{tl.b.name}_{j}", excl=True)


def build_program(phase_list=None, dbg_x=False):
    nc = bass.Bass("TRN2", target_bir_lowering=False)

    def din(name, shape):
        return nc.dram_tensor(name, list(shape), F32, kind="ExternalInput").ap()

    x_in = din("x", [NCH, 128, WT])
    gmix_d = din("gmix", [128, 4, 8])
    gffn_d = din("gffn", [128, 4, 8])
    poolw_d = din("pool_w", [2, 4, 256, 256])
    poolb_d = din("poolb", [128, 2, 8])
    pools_d = din("pools", [128, 2, 8])
    wqkv_d = din("attn_wqkv", [2, 1024, 3072])
    gq_d = din("gq", [128, 2])
    gk_d = din("gk", [128, 2])
    relb_d = din("relb", [2, 128, 16, 640])
    wo_d = din("attn_wo", [2, 1024, 1024])
    wg_d = din("ffn_w_gate", [4, 1024, FF])
    wv_d = din("ffn_w_val", [4, 1024, FF])
    convw_d = din("convw", [128, 4, NF, 3])
    convb_d = din("convb", [128, 4, NF])
    wout_d = din("ffn_w_out", [4, FF, 1024])
    premask_d = din("premask", [128, 1])
    invcnt_d = din("invcnt", [128, 4, 16])
    y_out = nc.dram_tensor("y", [NCH, 128, OWN], F32, kind="ExternalOutput").ap()
    xs = nc.dram_tensor("xs", [NCH, 128, WT], F32).ap()
    ktd = nc.dram_tensor("ktd", [NCH, 128, WT], BF16).ap()
    qtd = nc.dram_tensor("qtd", [NCH, 128, WT], BF16).ap()
    vad = nc.dram_tensor("vad", [WT, 2048], BF16).ap()
    xdbg = None
    if dbg_x:
        xdbg = nc.dram_tensor("xdbg", [NCH, 128, WT], F32, kind="ExternalOutput").ap()

    with ExitStack() as gst:
        S = Sched(nc, gst)
        PSALL = TL(nc, gst, "psum_all", [128, 8, 512], F32, psum=True)
        P = [PView(PSALL, i) for i in range(8)]
        PP = PSALL

        def G(name, shape, dtype=F32):
            return TL(nc, gst, "g_" + name, shape, dtype)

        ones_d = G("ones_d", [128, 128])
        ones_h = G("ones_h", [128, 128])
        tmp1 = G("tmp1", [128, 128])
        gmix = G("gmix", [128, 4, 8])
        gffn = G("gffn", [128, 4, 8])
        gmixm = G("gmixm", [128, 4, 8])
        gffnm = G("gffnm", [128, 4, 8])
        poolb = G("poolb", [128, 2, 8])
        pools = G("pools", [128, 2, 8])
        poolbs = G("poolbs", [128, 2, 8])
        convw = G("convw", [128, 4, NF, 3])
        convb = G("convb", [128, 4, NF])
        gq = G("gq", [128, 2])
        gk = G("gk", [128, 2])
        premask = G("premask", [128, 1])
        invcnt = G("invcnt", [128, 4, 16])
        cl = S.dsem("d_const")
        for tl, src in ((gmix, gmix_d), (gffn, gffn_d), (poolb, poolb_d), (pools, pools_d),
                        (convw, convw_d), (convb, convb_d), (gq, gq_d), (gk, gk_d),
                        (premask, premask_d), (invcnt, invcnt_d)):
            S.dma("sp", tl.t[:], src, writes=[tl.b], sem=cl)
        for tl in (gmix, gffn, poolb, pools, convw, convb, gq, gk, premask, invcnt):
            tl.b.w = (cl, S.cnt[cl])
        S.I("dve", "memset", writes=[tmp1.b], ap=tmp1.t[:], constant=1.0 / D)
        S.I("dve", "tensor_copy", reads=[tmp1.b], writes=[ones_d.b], out=ones_d.t[:].bitcast(F32R), in_=tmp1.t[:])
        S.I("dve", "memset", writes=[tmp1.b], ap=tmp1.t[:], constant=0.0)
        S.I("dve", "memset", writes=[tmp1.b], ap=tmp1.t[0:64, 0:64], constant=1.0 / 64)
        S.I("dve", "memset", writes=[tmp1.b], ap=tmp1.t[64:128, 64:128], constant=1.0 / 64)
        S.I("dve", "tensor_copy", reads=[tmp1.b], writes=[ones_h.b], out=ones_h.t[:].bitcast(F32R), in_=tmp1.t[:])
        S.I("dve", "tensor_scalar", reads=[gmix.b, premask.b], writes=[gmixm.b], out=gmixm.t[:], in0=gmix.t[:],
            scalar1=premask.t[:, 0:1], scalar2=None, op0=ALU.mult)
        S.I("dve", "tensor_scalar", reads=[gffn.b, premask.b], writes=[gffnm.b], out=gffnm.t[:], in0=gffn.t[:],
            scalar1=premask.t[:, 0:1], scalar2=None, op0=ALU.mult)
        S.I("dve", "tensor_tensor", reads=[poolb.b, pools.b], writes=[poolbs.b], out=poolbs.t[:], in0=poolb.t[:],
            in1=pools.t[:], op=ALU.mult)
        S.I("dve", "tensor_scalar", reads=[gq.b], writes=[gq.b], out=gq.t[:], in0=gq.t[:],
            scalar1=0.125, scalar2=None, op0=ALU.mult)

        C = dict(nc=nc, S=S, P=P, PP=PP, wg_d=wg_d, wv_d=wv_d, wout_d=wout_d, poolw_d=poolw_d, wqkv_d=wqkv_d, ones_d=ones_d, ones_h=ones_h, gmix=gmix, gffn=gffn, gmixm=gmixm,
                 gffnm=gffnm, pools=pools, poolbs=poolbs, convw=convw, convb=convb, gq=gq, gk=gk,
                 premask=premask, invcnt=invcnt)

        with ExitStack() as zst:
            zb = TL(nc, zst, "zero_bf", [128, NCH, 384], BF16)
            zf = TL(nc, zst, "zero_f", [128, NCH, 256], F32)
            S.I("dve", "memset", writes=[zb.b], ap=zb.t[:], constant=0.0)
            S.I("dve", "memset", writes=[zf.b], ap=zf.t[:], constant=0.0)
            zs = S.dsem("d_zero")
            S.dma("sp", ktd[:, :, 0:384].rearrange("c p n -> p c n"), zb.t[:], reads=[zb.b], sem=zs)
            S.dma("sp", qtd[:, :, 0:384].rearrange("c p n -> p c n"), zb.t[:], reads=[zb.b], sem=zs)
            zv = zb.t[:].rearrange("p c n -> p (c n)")
            for i in range(3):
                S.dma("sp", vad[i * 128:(i + 1) * 128, :], zv[:, 0:2048], reads=[zb.b], sem=zs)
            S.dma("sp", xs[:, :, 0:256].rearrange("c p n -> p c n"), zf.t[:], reads=[zf.b], sem=zs)
            S.barrier()

        phases = []
        for l in range(4):
            j = l // 2
            st_ = PHASE_STARTS[l]
            if l % 2 == 0:
                phases.append(("pool", l, j, st_["mix"]))
                phases.append(("ffn", l, st_["ffn"]))
            else:
                phases.append(("qkv", l, j, st_["mix"]))
                phases.append(("attn", l, j, st_["q"]))
                phases.append(("ffn", l, st_["ffn"]))
        full = phase_list is None
        if phase_list is not None:
            phases = phase_list
        src = x_in
        for pi, ph in enumerate(phases):
            last = (pi == len(phases) - 1) and full
            if pi > 0:
                S.barrier()
                S.new_phase()
            if ph[0] == "pool":
                phase_pool(C, ph[1], ph[2], ph[3], src, xs)
                src = xs
            elif ph[0] == "ffn":
                phase_ffn(C, ph[1], ph[2], src, xs, y_out if last else None)
            elif ph[0] == "qkv":
                phase_qkv(C, ph[1], ph[2], ph[3], src, ktd, qtd, vad)
            elif ph[0] == "attn":
                phase_attn(C, ph[1], ph[2], ph[3], src, xs, ktd, qtd, vad, relb_d, wo_d)
        if dbg_x:
            S.barrier()
            with ExitStack() as st:
                cp = TL(nc, st, "dbgcp", [128, 2, 4096], F32)
                ds = S.dsem("d_dbg")
                for c in range(NCH):
                    for hh in range(2):
                        n0 = hh * 4096
                        n1 = min(WT, n0 + 4096)
                        S.dma("sp", cp.t[:, hh, 0:n1 - n0], xs[c, :, n0:n1], writes=[cp.b], sem=ds)
                        S.dma("sp", xdbg[c, :, n0:n1], cp.t[:, hh, 0:n1 - n0], reads=[cp.b], sem=ds)
        S.finish("sp")
        S.finish("pool")
        STATS["nwaits"] = S.nwaits
        STATS["nissue"] = S.nissue
    return nc


def make_tiles(start):
    tiles = []
    t = start
    if t % TS:
        n = TS - t % TS
        tiles.append((t, n))
        t += n
    while t < WT:
        tiles.append((t, TS))
        t += TS
    return tiles


def emit_norm_stats(C, B):
    S, P = C["S"], C["P"]
    xt, sq, rstd = B["xt"], B["sq"], B["rstd"]
    n = B["n"]
    ps = P[0]
    for c in range(NCH):
        s = sq[c % 2]
        S.I("act", "activation", reads=[xt.b], writes=[s.b], out=s.t[:, 0:n].bitcast(F32R), in_=xt.t[:, c, 0:n], func=AF.Square)
        S.I("pe", "matmul", reads=[C["ones_d"].b, s.b], writes=[ps.b],
            out=ps.t[:, 0:n], lhsT=C["ones_d"].t[:].bitcast(F32R), rhs=s.t[:, 0:n].bitcast(F32R), start=(c == 0), stop=(c == NCH - 1))
    S.I("act", "activation", reads=[ps.b], writes=[rstd.b], out=rstd.t[:, 0:n], in_=ps.t[:, 0:n], func=AF.Ln, bias=EPS, scale=1.0)
    S.I("act", "activation", reads=[rstd.b], writes=[rstd.b], out=rstd.t[:, 0:n], in_=rstd.t[:, 0:n], func=AF.Exp, scale=-0.5)


def emit_norm_h(C, B, gt, l):
    S = C["S"]
    xt, rstd, h = B["xt"], B["rstd"], B["h"]
    hoff = B.get("hoff", 0)
    n = B["n"]
    for c in range(NCH):
        S.I("dve", "scalar_tensor_tensor", reads=[xt.b, rstd.b, gt.b], writes=[h.b],
            out=h.t[:, c, hoff:hoff + n], in0=xt.t[:, c, 0:n], scalar=gt.t[:, l, c:c + 1], in1=rstd.t[:, 0:n],
            op0=ALU.mult, op1=ALU.mult)


def emit_norm_b(C, B, t, gt, l):
    emit_norm_stats(C, B)
    emit_norm_h(C, B, gt, l)


def load_x_tile(C, B, tile, src):
    S = C["S"]
    xt = B["xt"]
    tok0, n = tile
    B["n"] = n
    S.dma("sp", xt.t[:, :, 0:n], src[:, :, tok0:tok0 + n].rearrange("c p n -> p c n"), writes=[xt.b],
          sem=S.dsem("d_x" + xt.b.name.split("_")[-1]))


def phase_ffn(C, l, start, src, dst, y_out):
    nc, S, P = C["nc"], C["S"], C["P"]
    OTOK = OT * TS
    with ExitStack() as st:
        def A(name, shape, dtype=F32):
            return TL(nc, st, f"f{l}_{name}", shape, dtype)
        wg = A("wg", [128, NCH, FF], BF16)
        wv = A("wv", [128, NCH, FF], BF16)
        wo = A("wo", [128, NF, D], BF16)
        B = dict(xt=A("xt", [128, NCH, TS]), sq=[A("sq0", [128, TS]), A("sq1", [128, TS])],
                 rstd=A("rstd", [128, TS]), h=A("hT", [128, NCH, TS], BF16))
        u = A("u", [128, NF, TS], BF16)
        asb = [A("asb0", [128, TS + 2]), A("asb1", [128, TS + 2])]
        t1 = [A("t1a", [128, TS]), A("t1b", [128, TS])]
        t2 = [A("t2a", [128, TS]), A("t2b", [128, TS])]
        carry = A("carry", [128, NF, 2])
        xr = [A("xr0", [128, TS]), A("xr1", [128, TS])]
        wgd = C["wg_d"][l].rearrange("(k p) f -> p k f", p=128)
        wvd = C["wv_d"][l].rearrange("(k p) f -> p k f", p=128)
        wod = C["wout_d"][l].rearrange("(j p) d -> p j d", p=128)
        FG = [(0, 6), (6, 12), (12, 17), (17, 22)]
        wg_b = [Buf(f"wg{g}") for g in range(4)]
        wv_b = [Buf(f"wv{g}") for g in range(4)]
        fgrp = {}
        for g, (f0, f1) in enumerate(FG):
            for f in range(f0, f1):
                fgrp[f] = g
            c0, c1 = f0 * 128, f1 * 128
            S.dma("pool", wg.t[:, :, c0:c1], wgd[:, :, c0:c1], writes=[wg_b[g]], sem=S.dsem(f"d_wg{g}"))
            S.dma("pool", wv.t[:, :, c0:c1], wvd[:, :, c0:c1], writes=[wv_b[g]], sem=S.dsem(f"d_wv{g}"))
        for jj in range(0, NF, 2):
            S.dma("pool", wo.t[:, jj:jj + 2, :], wod[:, jj:jj + 2, :], writes=[wo.b], sem=S.dsem("d_wo"))
        carry_b = [Buf() for _ in range(NF)]
        S.I("dve", "memset", writes=[carry.b], ap=carry.t[:], constant=0.0)
        for cbf in carry_b:
            cbf.w = carry.b.w
        cw, cb = C["convw"], C["convb"]

        def gsel(tile):
            return C["gffnm"] if tile[0] < OTOK else C["gffn"]

        tiles = make_tiles(start)
        load_x_tile(C, B, tiles[0], src)
        emit_norm_b(C, B, 0, gsel(tiles[0]), l)
        h = B["h"]
        for ti, (tok0, n) in enumerate(tiles):
            nxt = tiles[ti + 1] if ti + 1 < len(tiles) else None
            for f in range(NF):
                pg, pv = P[1 + f % 2], P[3 + f % 2]
                for k in range(NCH):
                    S.I("pe", "matmul", reads=[wg_b[fgrp[f]], h.b], writes=[pg.b], signal=(k == NCH - 1),
                        out=pg.t[:, 0:n], lhsT=wg.t[:, k, f * 128:(f + 1) * 128], rhs=h.t[:, k, 0:n], start=(k == 0), stop=(k == NCH - 1))
                for k in range(NCH):
                    S.I("pe", "matmul", reads=[wv_b[fgrp[f]], h.b], writes=[pv.b], signal=(k == NCH - 1),
                        out=pv.t[:, 0:n], lhsT=wv.t[:, k, f * 128:(f + 1) * 128], rhs=h.t[:, k, 0:n], start=(k == 0), stop=(k == NCH - 1))
                a = asb[f % 2]
                ta, tb = t1[f % 2], t2[f % 2]
                S.I("act", "activation", reads=[pg.b], writes=[a.b], out=a.t[:, 2:n + 2], in_=pg.t[:, 0:n], func=AF.Copy)
                S.I("act", "activation", reads=[pg.b, cw.b, cb.b], writes=[ta.b], out=ta.t[:, 0:n], in_=pg.t[:, 0:n], func=AF.Identity,
                    bias=cb.t[:, l, f:f + 1], scale=cw.t[:, l, f, 2:3])
                S.I("dve", "tensor_copy", reads=[carry_b[f]], writes=[a.b], out=a.t[:, 0:2], in_=carry.t[:, f, :])
                S.I("dve", "tensor_copy", reads=[a.b], writes=[carry_b[f]], out=carry.t[:, f, :], in_=a.t[:, n:n + 2])
                S.I("dve", "scalar_tensor_tensor", reads=[a.b, ta.b, cw.b], writes=[tb.b], out=tb.t[:, 0:n], in0=a.t[:, 0:n],
                    scalar=cw.t[:, l, f, 0:1], in1=ta.t[:, 0:n], op0=ALU.mult, op1=ALU.add)
                S.I("dve", "scalar_tensor_tensor", reads=[a.b, tb.b, cw.b], writes=[ta.b], out=ta.t[:, 0:n], in0=a.t[:, 1:n + 1],
                    scalar=cw.t[:, l, f, 1:2], in1=tb.t[:, 0:n], op0=ALU.mult, op1=ALU.add)
                S.I("act", "activation", reads=[ta.b], writes=[tb.b], out=tb.t[:, 0:n], in_=ta.t[:, 0:n], func=AF.Silu)
                S.I("dve", "tensor_tensor", reads=[tb.b, pv.b], writes=[u.b], out=u.t[:, f, 0:n], in0=tb.t[:, 0:n], in1=pv.t[:, 0:n], op=ALU.mult)
            if nxt is not None:
                load_x_tile(C, B, nxt, src)
            for dch in range(NCH):
                if dch == 4 and nxt is not None:
                    emit_norm_b(C, B, 0, gsel(nxt), l)
                po = P[5 + dch % 2]
                x_r = xr[dch % 2]
                S.dma("sp", x_r.t[:, 0:n], src[dch, :, tok0:tok0 + n], writes=[x_r.b], sem=S.dsem(f"d_xr{dch % 2}"))
                for f in range(NF):
                    S.I("pe", "matmul", reads=[wo.b, u.b], writes=[po.b], signal=(f == NF - 1),
                        out=po.t[:, 0:n], lhsT=wo.t[:, f, dch * 128:(dch + 1) * 128], rhs=u.t[:, f, 0:n], start=(f == 0), stop=(f == NF - 1))
                S.I("dve", "tensor_tensor", reads=[x_r.b, po.b], writes=[x_r.b], out=x_r.t[:, 0:n], in0=x_r.t[:, 0:n], in1=po.t[:, 0:n], op=ALU.add)
                if y_out is None:
                    S.dma("pool", dst[dch, :, tok0:tok0 + n], x_r.t[:, 0:n], reads=[x_r.b], sem=S.dsem(f"d_sx{dch % 2}"))
                elif tok0 >= OTOK:
                    S.dma("pool", y_out[dch, :, tok0 - OTOK:tok0 - OTOK + n], x_r.t[:, 0:n], reads=[x_r.b], sem=S.dsem(f"d_sx{dch % 2}"))


def phase_pool(C, l, j, start, src, dst):
    nc, S, P = C["nc"], C["S"], C["P"]
    HX = 16
    OTOK = OT * TS
    with ExitStack() as st:
        def A(name, shape, dtype=F32):
            return TL(nc, st, f"p{l}_{name}", shape, dtype)
        pw = A("pw", [128, NCH, 256], BF16)
        xts = [A("xt0", [128, NCH, TS]), A("xt1", [128, NCH, TS])]
        B = dict(xt=xts[0], sq=[A("sq0", [128, TS]), A("sq1", [128, TS])],
                 rstd=A("rstd", [128, TS]), h=A("hx", [128, NCH, HX + TS]), hoff=HX)
        sA = A("sA", [128, NCH, HX + TS])
        sB = A("sB", [128, NCH, HX + TS])
        dT = A("dT", [128, NCH, TS], BF16)
        NOST = 6
        ost = [A(f"ost{i}", [128, TS]) for i in range(NOST)]
        tfix = A("tfix", [128, 16])
        pwd = C["poolw_d"][j].rearrange("g (jj p) d -> p g jj d", p=128)
        for g in range(4):
            S.dma("pool", pw.t[:, 2 * g:2 * g + 2, :], pwd[:, g, :, :], writes=[pw.b], sem=S.dsem("d_pw"))
        hx = B["h"]
        S.I("dve", "memset", writes=[hx.b], ap=hx.t[:, :, 0:HX], constant=0.0)

        def gsel(tile):
            return C["gmixm"] if tile[0] < OTOK else C["gmix"]

        tiles = make_tiles(start)
        B["xt"] = xts[0]
        load_x_tile(C, B, tiles[0], src)
        emit_norm_stats(C, B)
        n_prev = None
        for ti, (tok0, n) in enumerate(tiles):
            nxt = tiles[ti + 1] if ti + 1 < len(tiles) else None
            NW = HX + n
            B["xt"] = xts[ti % 2]
            B["n"] = n
            if ti > 0:
                S.I("dve", "tensor_copy", reads=[hx.b], writes=[hx.b], out=hx.t[:, :, 0:HX], in_=hx.t[:, :, n_prev:n_prev + HX])
            emit_norm_h(C, B, gsel((tok0, n)), l)
            if nxt is not None:
                B["xt"] = xts[(ti + 1) % 2]
                load_x_tile(C, B, nxt, src)
                emit_norm_stats(C, B)
                B["xt"] = xts[ti % 2]
                B["n"] = n
            S.I("dve", "tensor_tensor", reads=[hx.b], writes=[sA.b], out=sA.t[:, :, 1:NW], in0=hx.t[:, :, 1:NW], in1=hx.t[:, :, 0:NW - 1], op=ALU.add)
            S.I("dve", "tensor_tensor", reads=[sA.b], writes=[sB.b], out=sB.t[:, 2:8, 3:NW], in0=sA.t[:, 2:8, 3:NW], in1=sA.t[:, 2:8, 1:NW - 2], op=ALU.add)
            S.I("dve", "tensor_tensor", reads=[sB.b], writes=[sA.b], out=sA.t[:, 4:8, 7:NW], in0=sB.t[:, 4:8, 7:NW], in1=sB.t[:, 4:8, 3:NW - 4], op=ALU.add)
            S.I("dve", "tensor_tensor", reads=[sA.b], writes=[sB.b], out=sB.t[:, 6:8, 15:NW], in0=sA.t[:, 6:8, 15:NW], in1=sA.t[:, 6:8, 7:NW - 8], op=ALU.add)
            srcs = [sA, sB, sA, sB]
            for g in range(4):
                w = 2 << g
                sg = srcs[g]
                S.I("dve", "scalar_tensor_tensor", reads=[sg.b, hx.b], writes=[dT.b], out=dT.t[:, 2 * g:2 * g + 2, 0:n],
                    in0=sg.t[:, 2 * g:2 * g + 2, HX:NW], scalar=1.0 / w, in1=hx.t[:, 2 * g:2 * g + 2, HX:NW],
                    op0=ALU.mult, op1=ALU.subtract)
            if tok0 == OTOK:
                ic = C["invcnt"]
                for g in range(4):
                    sg = srcs[g]
                    for c in (2 * g, 2 * g + 1):
                        S.I("dve", "tensor_tensor", reads=[sg.b, ic.b], writes=[tfix.b], out=tfix.t[:], in0=sg.t[:, c, HX:HX + 16],
                            in1=ic.t[:, g, :], op=ALU.mult)
                        S.I("dve", "tensor_tensor", reads=[tfix.b, hx.b], writes=[dT.b], out=dT.t[:, c, 0:16], in0=tfix.t[:],
                            in1=hx.t[:, c, HX:HX + 16], op=ALU.subtract)
            xt = B["xt"]
            for g in range(4):
                for jo in range(2):
                    co = 2 * g + jo
                    po = P[1 + co % 4]
                    for ji in range(2):
                        S.I("pe", "matmul", reads=[pw.b, dT.b], writes=[po.b], signal=(ji == 1), out=po.t[:, 0:n],
                            lhsT=pw.t[:, 2 * g + ji, jo * 128:(jo + 1) * 128], rhs=dT.t[:, 2 * g + ji, 0:n], start=(ji == 0), stop=(ji == 1))
                    o = ost[(ti * 8 + co) % NOST]
                    S.I("act", "activation", reads=[po.b, C["pools"].b, C["poolbs"].b], writes=[o.b], out=o.t[:, 0:n], in_=po.t[:, 0:n],
                        func=AF.Identity, bias=C["poolbs"].t[:, j, co:co + 1], scale=C["pools"].t[:, j, co:co + 1])
                    S.I("dve", "tensor_tensor", reads=[o.b, xt.b], writes=[o.b], out=o.t[:, 0:n], in0=o.t[:, 0:n], in1=xt.t[:, co, 0:n], op=ALU.add)
                    S.dma("pool", dst[co, :, tok0:tok0 + n], o.t[:, 0:n], reads=[o.b], sem=S.dsem(f"d_so{(ti * 8 + co) % NOST}"))
            n_prev = n


def phase_qkv(C, l, j, start, src, ktd, qtd, vad):
    nc, S, P = C["nc"], C["S"], C["P"]
    with ExitStack() as st:
        def A(name, shape, dtype=F32):
            return TL(nc, st, f"q{l}_{name}", shape, dtype)
        wq = A("wqkv", [128, NCH, 3 * D], BF16)
        B = dict(xt=A("xt", [128, NCH, TS]), sq=[A("sq0", [128, TS]), A("sq1", [128, TS])],
                 rstd=A("rstd", [128, TS]), h=None)
        hTs = [A("hT0", [128, NCH, TS], BF16), A("hT1", [128, NCH, TS], BF16)]
        sqh = [A("sqh0", [128, TS]), A("sqh1", [128, TS])]
        rh = [A("rh0", [128, TS]), A("rh1", [128, TS])]
        qo = [A(f"qo{i}", [128, TS], BF16) for i in range(3)]
        vo = [A(f"vo{i}", [128, 8, 2, 128], BF16) for i in range(2)]
        vop = [A(f"vop{i}", [128, 8, 2, 128], BF16) for i in range(2)]
        wqd = C["wqkv_d"][j].rearrange("(k p) f -> p k f", p=128)
        wq_b = [Buf(f"wq{g}") for g in range(6)]
        for g in range(6):
            S.dma("pool", wq.t[:, :, g * 512:(g + 1) * 512], wqd[:, :, g * 512:(g + 1) * 512], writes=[wq_b[g]], sem=S.dsem(f"d_wq{g}"))
        pm = C["premask"]
        for v in vo + vop:
            S.I("dve", "memset", writes=[v.b], ap=v.t[:], constant=1.0)
        for v in vop:
            S.I("dve", "tensor_scalar", reads=[v.b, pm.b], writes=[v.b], out=v.t[:], in0=v.t[:], scalar1=pm.t[:, 0:1],
                scalar2=None, op0=ALU.mult)

        OTOK = OT * TS

        def gsel(tile):
            return C["gmixm"] if tile[0] < OTOK else C["gmix"]

        PQ_RING = [P[1], P[2], P[3], P[6], P[7]]
        tiles = make_tiles(start)
        B["h"] = hTs[0]
        load_x_tile(C, B, tiles[0], src)
        emit_norm_b(C, B, 0, gsel(tiles[0]), l)
        for ti, (tok0, n) in enumerate(tiles):
            nxt = tiles[ti + 1] if ti + 1 < len(tiles) else None
            h = hTs[ti % 2]
            cols = slice(tok0, tok0 + n)
            pend = None

            def finish(pend):
                m, pq, s_ = pend
                ph = P[4 + m % 2]
                r_ = rh[m % 2]
                S.I("pe", "matmul", reads=[C["ones_h"].b, s_.b], writes=[ph.b], out=ph.t[:, 0:n], lhsT=C["ones_h"].t[:].bitcast(F32R),
                    rhs=s_.t[:, 0:n].bitcast(F32R), start=True, stop=True)
                S.I("act", "activation", reads=[ph.b], writes=[r_.b], out=r_.t[:, 0:n], in_=ph.t[:, 0:n], func=AF.Ln, bias=EPS, scale=1.0)
                S.I("act", "activation", reads=[r_.b], writes=[r_.b], out=r_.t[:, 0:n], in_=r_.t[:, 0:n], func=AF.Exp, scale=-0.5)
                o = qo[m % 3]
                gsc = C["gq"] if m < 8 else C["gk"]
                S.I("dve", "scalar_tensor_tensor", reads=[pq.b, r_.b, gsc.b], writes=[o.b], out=o.t[:, 0:n], in0=pq.t[:, 0:n],
                    scalar=gsc.t[:, j:j + 1], in1=r_.t[:, 0:n], op0=ALU.mult, op1=ALU.mult)
                dstd = qtd if m < 8 else ktd
                S.dma("pool", dstd[m % 8, :, cols], o.t[:, 0:n], reads=[o.b], sem=S.dsem(f"d_qo{m % 3}"))

            for m in range(16):
                pq = PQ_RING[m % 5]
                for k in range(NCH):
                    S.I("pe", "matmul", reads=[wq_b[m // 4], h.b], writes=[pq.b], signal=(k == NCH - 1), out=pq.t[:, 0:n],
                        lhsT=wq.t[:, k, m * 128:(m + 1) * 128], rhs=h.t[:, k, 0:n], start=(k == 0), stop=(k == NCH - 1))
                s_ = sqh[m % 2]
                S.I("act", "activation", reads=[pq.b], writes=[s_.b], out=s_.t[:, 0:n].bitcast(F32R), in_=pq.t[:, 0:n], func=AF.Square)
                if pend is not None:
                    finish(pend)
                pend = (m, pq, s_)
            if nxt is not None:
                B["h"] = hTs[(ti + 1) % 2]
                load_x_tile(C, B, nxt, src)
                emit_norm_b(C, B, 0, gsel(nxt), l)
            vbufs = vop if tok0 < OTOK else vo
            for tb in range(n // 128):
                v = vbufs[tb % 2]
                for cb in range(2):
                    pvv = P[6 + cb]
                    for k in range(NCH):
                        S.I("pe", "matmul", reads=[wq_b[4 + cb], h.b], writes=[pvv.b], signal=(k == NCH - 1), out=pvv.t[:],
                            lhsT=h.t[:, k, tb * 128:(tb + 1) * 128], rhs=wq.t[:, k, 2 * D + cb * 512:2 * D + (cb + 1) * 512],
                            start=(k == 0), stop=(k == NCH - 1))
                    if pend is not None:
                        finish(pend)
                        pend = None
                    pr = pvv.t[:].rearrange("p (i e d) -> p i e d", e=2, d=64)
                    S.I("act", "activation", reads=[pvv.b], writes=[v.b], out=v.t[:, cb * 4:(cb + 1) * 4, 0, 0:64], in_=pr[:, :, 0, :], func=AF.Copy)
                    S.I("act", "activation", reads=[pvv.b], writes=[v.b], out=v.t[:, cb * 4:(cb + 1) * 4, 1, 64:128], in_=pr[:, :, 1, :], func=AF.Copy)
                r0 = tok0 + tb * 128
                S.dma("pool", vad[r0:r0 + 128, :], v.t[:].rearrange("p a b c -> p (a b c)"), reads=[v.b],
                      sem=S.dsem(f"d_vo{tb % 2}{'p' if tok0 < OTOK else ''}"))


def phase_attn(C, l, j, q_tok, src, dst, ktd, qtd, vad, relb_d, wo_d):
    nc, S, P, PP = C["nc"], C["S"], C["P"], C["PP"]
    NPAIR = N_HEADS // 2
    tq = q_tok // TS
    cmin0 = (q_tok % TS) // 64
    with ExitStack() as st:
        def A(name, shape, dtype=F32):
            return TL(nc, st, f"a{l}_{name}", shape, dtype)
        E = A("E", [128, N_HEADS, 640])
        wo = A("wo", [128, NCH, D], BF16)
        Q = [A(f"Q{i}", [128, NCH, TS], BF16) for i in range(2)]
        Kt = [A(f"K{i}", [128, NCH, TS], BF16) for i in range(3)]
        Vt = [A(f"V{i}", [128, 4, 2048], BF16) for i in range(3)]
        NS = 4
        bank_ptr = [0]
        pexp = [A(f"pexp{i}", [128, 2, TS]) for i in range(NS)]
        pT = [A(f"pT{i}", [128, 2, TS], BF16) for i in range(NS + 1)]
        rden = [A(f"rden{i}", [128, 2, TS]) for i in range(2)]
        accs = [A(f"accs{i}", [128, 2, TS]) for i in range(2)]
        oT = A("oT", [128, NCH, TS], BF16)
        NXR = 4
        xr = [A(f"xr{i}", [128, TS]) for i in range(NXR)]
        E_b = [Buf(f"E{g}") for g in range(4)]
        for g in range(4):
            hh = 4 * g
            S.dma("sp", E.t[:, hh:hh + 4, :], relb_d[j, :, hh:hh + 4, :], writes=[E_b[g]], sem=S.dsem(f"d_E{g}"))
            S.I("act", "activation", reads=[E_b[g]], writes=[E_b[g]], out=E.t[:, hh:hh + 4, :], in_=E.t[:, hh:hh + 4, :], func=AF.Exp)
        wod = wo_d[j].rearrange("(k p) d -> p k d", p=128)
        for k in range(0, NCH, 2):
            S.dma("pool", wo.t[:, k:k + 2, :], wod[:, k:k + 2, :], writes=[wo.b], sem=S.dsem("d_awo"))

        def load_kv(t):
            kt_, vt_ = Kt[t % 3], Vt[t % 3]
            S.dma("sp", kt_.t[:], ktd[:, :, t * TS:(t + 1) * TS].rearrange("c p n -> p c n"), writes=[kt_.b], sem=S.dsem(f"d_K{t % 3}"))
            S.dma("sp", vt_.t[:], vad[t * TS:(t + 1) * TS, :].rearrange("(kt p) f -> p kt f", p=128), writes=[vt_.b], sem=S.dsem(f"d_V{t % 3}"))

        def load_q(t):
            q_ = Q[t % 2]
            S.dma("sp", q_.t[:], qtd[:, :, t * TS:(t + 1) * TS].rearrange("c p n -> p c n"), writes=[q_.b], sem=S.dsem(f"d_Q{t % 2}"))

        load_kv(tq - 1)
        load_kv(tq)
        load_q(tq)
        step = 0
        for bt in range(tq, NT):
            if bt + 1 < NT:
                load_kv(bt + 1)
                load_q(bt + 1)
            q_ = Q[bt % 2]
            pending = []
            norm_q = []
            cmin = cmin0 if bt == tq else 0
            cs = cmin * 64
            kts = []
            for jk in KT_ORDER:
                qc0 = max(0, 2 * jk - 8, cmin)
                qc1 = min(8, 2 * jk + 2)
                if qc1 > qc0:
                    kts.append((jk, qc0, qc1))
            nk = len(kts)

            def do_evac(c):
                ac = accs[c % 2]
                S.I("act", "activation", reads=[P[6].b, P[7].b], writes=[ac.b], out=ac.t[:, :, cs:TS], in_=bank_pair(PP, 6).t[:, :, cs:TS], func=AF.Copy)

            def norm_tasks(c):
                ac = accs[c % 2]
                rd = rden[c % 2]

                def ln(hp):
                    dr = slice((1 - hp) * 64, (1 - hp) * 64 + 64)
                    S.I("act", "activation", reads=[ac.b], writes=[rd.b], out=rd.t[dr, hp, cs:TS], in_=ac.t[dr, hp, cs:TS], func=AF.Ln, bias=1e-30, scale=1.0)

                def fin(hp):
                    nr = slice(hp * 64, hp * 64 + 64)
                    dr = slice((1 - hp) * 64, (1 - hp) * 64 + 64)
                    S.I("act", "activation", reads=[rd.b], writes=[rd.b], out=rd.t[nr, hp, cs:TS], in_=rd.t[dr, hp, cs:TS], func=AF.Exp, scale=-1.0)
                    S.I("dve", "tensor_tensor", reads=[ac.b, rd.b], writes=[oT.b], out=oT.t[nr, c, cs:TS], in0=ac.t[nr, hp, cs:TS], in1=rd.t[nr, hp, cs:TS], op=ALU.mult)

                return [lambda: ln(0), lambda: ln(1), lambda: fin(0), lambda: fin(1)]

            def do_pv(item):
                (c, idx, jk, n, qs, pt) = item
                last = (idx == nk - 1)
                acc = bank_pair(PP, 6)
                accb = [P[6].b, P[7].b]
                tsrc = bt - 1 if jk < 4 else bt
                vt_ = Vt[tsrc % 3]
                for hp in range(2):
                    h = 2 * c + hp
                    S.I("pe", "matmul", reads=[vt_.b, pt.b], writes=accb, signal=(last and hp == 1), out=acc.t[:, hp, qs:qs + n],
                        lhsT=vt_.t[:, jk % 4, h * 128:(h + 1) * 128], rhs=pt.t[:, hp, 0:n], start=(idx == 0), stop=last,
                        skip_group_check=True)
                if last:
                    do_evac(c)
                    norm_q.extend(norm_tasks(c))

            for c in range(NPAIR):
                for idx, (jk, qc0, qc1) in enumerate(kts):
                    n = (qc1 - qc0) * 64
                    qs = qc0 * 64
                    c0 = qs + 512 - 128 * jk
                    tsrc = bt - 1 if jk < 4 else bt
                    kt_ = Kt[tsrc % 3]
                    kc = (jk % 4) * 128
                    nb = 1 if n <= 256 else 2
                    if nb == 2 and bank_ptr[0] == 5:
                        bank_ptr[0] = 0
                    b0 = bank_ptr[0]
                    bank_ptr[0] = (b0 + nb) % 6
                    psb = [P[b0 + i].b for i in range(nb)]
                    if nb == 2:
                        ps_h = [P[b0].t[:, 0:n], P[b0 + 1].t[:, 0:n]]
                        ps_all = bank_pair(PP, b0).t[:, :, 0:n]
                    else:
                        ps_h = [P[b0].t[:, 0:n], P[b0].t[:, 256:256 + n]]
                        ps_all = P[b0].t[:].rearrange("p (h n) -> p h n", h=2)[:, :, 0:n]
                    for hp in range(2):
                        pr = slice(hp * 64, hp * 64 + 64)
                        if nb == 1 and hp == 1:
                            S.wait("pe", [psb[0].w])
                        S.I("pe", "matmul", reads=[kt_.b, q_.b], writes=psb, signal=(hp == 1 or nb == 1), out=ps_h[hp], lhsT=kt_.t[pr, c, kc:kc + 128],
                            rhs=q_.t[pr, c, qs:qs + n], start=True, stop=True)
                    pe_ = pexp[step % NS]
                    pt = pT[step % (NS + 1)]
                    S.I("act", "activation", reads=psb, writes=[pe_.b], out=pe_.t[:, :, 0:n], in_=ps_all, func=AF.Exp)
                    S.I("dve", "tensor_tensor", reads=[pe_.b, E_b[c // 2]], writes=[pt.b], out=pt.t[:, :, 0:n], in0=pe_.t[:, :, 0:n],
                        in1=E.t[:, 2 * c:2 * c + 2, c0:c0 + n], op=ALU.mult)
                    pending.append((c, idx, jk, n, qs, pt))
                    if len(pending) > NS - 1:
                        do_pv(pending.pop(0))
                    if norm_q:
                        norm_q.pop(0)()
                    step += 1
            while pending:
                do_pv(pending.pop(0))
            while norm_q:
                norm_q.pop(0)()
            for dch in range(NCH):
                pw_ = P[dch % 2]
                x_r = xr[dch % NXR]
                S.dma("sp", x_r.t[:, cs:TS], src[dch, :, bt * TS + cs:(bt + 1) * TS], writes=[x_r.b], sem=S.dsem(f"d_xr{dch % NXR}"))
                for k in range(NCH):
                    S.I("pe", "matmul", reads=[wo.b, oT.b], writes=[pw_.b], signal=(k == NCH - 1), out=pw_.t[:, cs:TS],
                        lhsT=wo.t[:, k, dch * 128:(dch + 1) * 128], rhs=oT.t[:, k, cs:TS], start=(k == 0), stop=(k == NCH - 1))
                S.I("dve", "tensor_tensor", reads=[x_r.b, pw_.b], writes=[x_r.b], out=x_r.t[:, cs:TS], in0=x_r.t[:, cs:TS], in1=pw_.t[:, cs:TS], op=ALU.add)
                S.dma("pool", dst[dch, :, bt * TS + cs:(bt + 1) * TS], x_r.t[:, cs:TS], reads=[x_r.b], sem=S.dsem(f"d_sx{dch % NXR}"))


HALO = OT * TS


def _chunked(v):
    v = np.asarray(v, np.float32)
    lead = v.shape[:-1]
    n = v.shape[-1] // 128
    v = v.reshape(lead + (n, 128))
    return np.ascontiguousarray(np.moveaxis(v, -1, 0))


def prepare_inputs(x, mix_norm, ffn_norm, pool_w, pool_b, pool_scale, attn_wqkv, attn_q_norm,
                   attn_k_norm, attn_rel_bias, attn_wo, ffn_w_gate, ffn_w_val, ffn_conv_w,
                   ffn_conv_b, ffn_w_out):
    f32 = np.float32
    x = np.asarray(x, f32)
    shared = {
        "gmix": _chunked(mix_norm), "gffn": _chunked(ffn_norm),
        "pool_w": np.ascontiguousarray(pool_w, f32),
        "poolb": _chunked(pool_b), "pools": _chunked(pool_scale),
        "attn_wqkv": np.ascontiguousarray(attn_wqkv, f32),
        "attn_wo": np.ascontiguousarray(attn_wo, f32),
        "ffn_w_gate": np.ascontiguousarray(ffn_w_gate, f32),
        "ffn_w_val": np.ascontiguousarray(ffn_w_val, f32),
        "ffn_w_out": np.ascontiguousarray(ffn_w_out, f32),
    }
    cw = np.asarray(ffn_conv_w, f32).reshape(4, 3, NF, 128)
    shared["convw"] = np.ascontiguousarray(cw.transpose(3, 0, 2, 1))
    shared["convb"] = _chunked(ffn_conv_b)
    gq = np.asarray(attn_q_norm, f32)
    gk = np.asarray(attn_k_norm, f32)
    shared["gq"] = np.ascontiguousarray(np.concatenate([gq, gq], axis=1).T)
    shared["gk"] = np.ascontiguousarray(np.concatenate([gk, gk], axis=1).T)
    p = np.arange(128)[:, None]
    c = np.arange(640)[None, :]
    idx = np.clip(c - p, -256, 256) + 256
    dchunk = c // 64 - p // 64
    valid = (dchunk >= 0) & (dchunk <= 8)
    rb = np.asarray(attn_rel_bias, f32)
    g = rb[:, :, idx]
    g = np.where(valid[None, None], g, f32(-30000.0))
    shared["relb"] = np.ascontiguousarray(g.transpose(0, 2, 1, 3))
    in_maps = []
    B, Sq, _ = x.shape
    for core in range(8):
        b, half = core // 2, core % 2
        s0 = half * OWN
        xw = np.zeros((WT, D), f32)
        lo = s0 - HALO
        if lo >= 0:
            xw[:] = x[b, lo:s0 + OWN]
        else:
            xw[HALO:] = x[b, 0:OWN]
        m = dict(shared)
        m["x"] = np.ascontiguousarray(xw.T.reshape(NCH, 128, WT))
        m["premask"] = np.full((128, 1), 1.0 if half else 0.0, f32)
        ic = np.zeros((128, 4, 16), f32)
        for gi, w in enumerate((2, 4, 8, 16)):
            if half:
                ic[:, gi, :] = 1.0 / w
            else:
                ic[:, gi, :] = 1.0 / np.minimum(np.arange(16) + 1, w)
        m["invcnt"] = ic
        in_maps.append(m)
    return in_maps


_NC_CACHE = {}


def kernel(**inputs):
    in_maps = prepare_inputs(**inputs)
    if "full" not in _NC_CACHE:
        _NC_CACHE["full"] = build_program()
    nc = _NC_CACHE["full"]
    res = run_bass_kernel_spmd(nc, in_maps, core_ids=list(range(8)))
    x = inputs["x"]
    out = np.empty(x.shape, np.float32)
    for core in range(8):
        b, half = core // 2, core % 2
        y = res.results[core]["y"].reshape(D, OWN)
        out[b, half * OWN:(half + 1) * OWN, :] = y.T
    return out
```
